# Optimizing a Trainium2 kernel written in Bass

```python
import math
import jax, jax.numpy as jnp
from jax import lax
import numpy as np

D_MODEL = 1024
BATCH = 8
SEQ = 2048
DEPTH = 1
DEC_BATCH = 128
DEC_SEQ = 8
PAST_LEN = 16384
PAGE_SIZE = 128

MIX_WIDTH = D_MODEL
A_HEADS = 4
A_HEAD_DIM = MIX_WIDTH // 2 // A_HEADS
A_WIDTH = A_HEADS * A_HEAD_DIM
B_HEADS = 4
B_HEAD_DIM = MIX_WIDTH // 2 // B_HEADS
B_WIDTH = B_HEADS * B_HEAD_DIM
IN_COLS = 4 * A_WIDTH + 4 * B_WIDTH
D_FF = ((-(-8 * D_MODEL // 3)) + 255) // 256 * 256
PLE_DIM = 256
CHUNK = 32
ROPE_BASE = 10000.0
NORM_EPS = 1e-5
DN_ALPHA = (2.0 * DEPTH) ** 0.25
DN_BETA = (8.0 * DEPTH) ** -0.25

kernel_name = "hymba_hgrn2_retention_deepnorm_step"


def layer_norm(x, g, b):
    x32 = x.astype(jnp.float32)
    mu = jnp.mean(x32, -1, keepdims=True)
    var = jnp.mean(jnp.square(x32 - mu), -1, keepdims=True)
    return ((x32 - mu) * lax.rsqrt(var + NORM_EPS) * g + b).astype(x.dtype)


def rope(x, pos):
    half = x.shape[-1] // 2
    inv = ROPE_BASE ** (-jnp.arange(half, dtype=jnp.float32) / half)
    ang = pos[:, None] * inv[None, :]
    cos = jnp.cos(ang)[None, :, None, :]
    sin = jnp.sin(ang)[None, :, None, :]
    x1, x2 = x[..., :half], x[..., half:]
    return jnp.concatenate([x1 * cos - x2 * sin, x1 * sin + x2 * cos], -1)


def gated_recurrence(q, k, v, logf, s0):
    Bn, L, H, K = q.shape
    V = v.shape[-1]
    C = math.gcd(L, CHUNK)
    N = L // C
    rs = lambda a: a.reshape(Bn, N, C, H, a.shape[-1])
    q, k, v, logf = rs(q), rs(k), rs(v), rs(logf)
    b = jnp.cumsum(logf, axis=2)
    b_last = b[:, :, -1:]
    q_dec = q * jnp.exp(b)
    k_dec = k * jnp.exp(-b)
    k_to_end = k * jnp.exp(b_last - b)
    causal = jnp.tril(jnp.ones((C, C), dtype=bool))
    scores = jnp.einsum('bnthk,bnshk->bnhts', q_dec, k_dec)
    scores = jnp.where(causal, scores, 0.0)
    intra = jnp.einsum('bnhts,bnshv->bnthv', scores, v)
    chunk_kv = jnp.einsum('bnshk,bnshv->bnhkv', k_to_end, v)
    chunk_decay = jnp.exp(b_last[:, :, 0])

    def step(s, inp):
        dec, kv, qd = inp
        inter = jnp.einsum('bchk,bhkv->bchv', qd, s)
        return dec[..., None] * s + kv, inter

    s_final, inter = lax.scan(step, s0, (jnp.moveaxis(chunk_decay, 1, 0), jnp.moveaxis(chunk_kv, 1, 0), jnp.moveaxis(q_dec, 1, 0)))
    inter = jnp.moveaxis(inter, 0, 1)
    o = (intra + inter).reshape(Bn, L, H, V)
    return o, s_final


def token_mixer(x, pos, lb, s_a, s_b, w_in, a_norm_g, b_norm_g, b_norm_b, w_out):
    Bn, L, _ = x.shape
    f32 = jnp.float32
    proj = (x @ w_in).astype(f32)
    cuts = [A_WIDTH, 2 * A_WIDTH, 3 * A_WIDTH, 4 * A_WIDTH, 4 * A_WIDTH + B_WIDTH, 4 * A_WIDTH + 2 * B_WIDTH, 4 * A_WIDTH + 3 * B_WIDTH]
    qa, fa, ia, ga, qb, kb, vb, gb = jnp.split(proj, cuts, axis=-1)
    heads_a = lambda t: t.reshape(Bn, L, A_HEADS, A_HEAD_DIM)
    heads_b = lambda t: t.reshape(Bn, L, B_HEADS, B_HEAD_DIM)

    f = lb + (1.0 - lb) * jax.nn.sigmoid(fa)
    o_a, s_a_new = gated_recurrence(heads_a(jax.nn.silu(qa)), heads_a(1.0 - f), heads_a(ia), heads_a(jnp.log(f)), s_a.astype(f32))
    o_a = o_a * lax.rsqrt(jnp.mean(jnp.square(o_a), -1, keepdims=True) + NORM_EPS) * a_norm_g.reshape(A_HEADS, A_HEAD_DIM)
    o_a = o_a.reshape(Bn, L, A_WIDTH) * jax.nn.silu(ga)

    log_decay = jnp.log1p(-(2.0 ** (-5.0 - jnp.arange(B_HEADS, dtype=f32))))
    qr = rope(heads_b(qb), pos)
    kr = rope(heads_b(kb), pos) * (B_HEAD_DIM ** -0.5)
    logd = jnp.broadcast_to(log_decay[:, None], (Bn, L, B_HEADS, B_HEAD_DIM))
    o_b, s_b_new = gated_recurrence(qr, kr, heads_b(vb), logd, s_b.astype(f32))
    mu = jnp.mean(o_b, -1, keepdims=True)
    var = jnp.mean(jnp.square(o_b - mu), -1, keepdims=True)
    o_b = (o_b - mu) * lax.rsqrt(var + NORM_EPS) * b_norm_g.reshape(B_HEADS, B_HEAD_DIM) + b_norm_b.reshape(B_HEADS, B_HEAD_DIM)
    o_b = o_b.reshape(Bn, L, B_WIDTH) * jax.nn.silu(gb)

    mix = jnp.concatenate([o_a, o_b], -1).astype(x.dtype) @ w_out
    return mix, s_a_new.astype(s_a.dtype), s_b_new.astype(s_b.dtype)


def decoder_layer(x, p, pos, lb, s_a, s_b, w_in, a_norm_g, b_norm_g, b_norm_b, w_out, ln1_g, ln1_b, w_ffn_gate, w_ffn_up, w_ffn_down, ln2_g, ln2_b, w_ple_proj, w_ple_gate, b_ple_gate):
    mix, s_a_new, s_b_new = token_mixer(x, pos, lb, s_a, s_b, w_in, a_norm_g, b_norm_g, b_norm_b, w_out)
    h = layer_norm(DN_ALPHA * x + mix, ln1_g, ln1_b)
    ffn = (jax.nn.silu(h @ w_ffn_gate) * (h @ w_ffn_up)) @ w_ffn_down
    h = layer_norm(DN_ALPHA * h + ffn, ln2_g, ln2_b)
    gate = jax.nn.sigmoid(h @ w_ple_gate + b_ple_gate)
    y = h + gate * (p.astype(h.dtype) @ w_ple_proj)
    return y, s_a_new, s_b_new


def setup_inputs(seed: int = 0) -> dict:
    key = jax.random.key(seed)
    ks = jax.random.split(key, 24)
    nrm = lambda k, shape, scale: jax.random.normal(k, shape, jnp.float32) * scale
    return {
        "x_prompt": nrm(ks[0], (BATCH, SEQ, D_MODEL), 1.0),
        "x_sample": nrm(ks[1], (DEC_BATCH, DEC_SEQ, D_MODEL), 1.0),
        "p_prompt": nrm(ks[2], (DEPTH, BATCH, SEQ, PLE_DIM), 1.0),
        "p_sample": nrm(ks[3], (DEPTH, DEC_BATCH, DEC_SEQ, PLE_DIM), 1.0),
        "state_hgrn": nrm(ks[4], (DEPTH, DEC_BATCH, A_HEADS, A_HEAD_DIM, A_HEAD_DIM), 0.5),
        "state_ret": nrm(ks[5], (DEPTH, DEC_BATCH, B_HEADS, B_HEAD_DIM, B_HEAD_DIM), 0.5),
        "lb_logits": nrm(ks[6], (DEPTH + 1, A_WIDTH), 0.1),
        "w_in": nrm(ks[7], (DEPTH, D_MODEL, IN_COLS), D_MODEL ** -0.5),
        "a_norm_g": 1.0 + nrm(ks[8], (DEPTH, A_WIDTH), 0.01),
        "b_norm_g": 1.0 + nrm(ks[9], (DEPTH, B_WIDTH), 0.01),
        "b_norm_b": nrm(ks[10], (DEPTH, B_WIDTH), 0.01),
        "w_out": nrm(ks[11], (DEPTH, MIX_WIDTH, D_MODEL), DN_BETA * MIX_WIDTH ** -0.5),
        "ln1_g": 1.0 + nrm(ks[12], (DEPTH, D_MODEL), 0.01),
        "ln1_b": nrm(ks[13], (DEPTH, D_MODEL), 0.01),
        "w_ffn_gate": nrm(ks[14], (DEPTH, D_MODEL, D_FF), D_MODEL ** -0.5),
        "w_ffn_up": nrm(ks[15], (DEPTH, D_MODEL, D_FF), D_MODEL ** -0.5),
        "w_ffn_down": nrm(ks[16], (DEPTH, D_FF, D_MODEL), DN_BETA * D_FF ** -0.5),
        "ln2_g": 1.0 + nrm(ks[17], (DEPTH, D_MODEL), 0.01),
        "ln2_b": nrm(ks[18], (DEPTH, D_MODEL), 0.01),
        "w_ple_proj": nrm(ks[19], (DEPTH, PLE_DIM, D_MODEL), PLE_DIM ** -0.5),
        "w_ple_gate": nrm(ks[20], (DEPTH, D_MODEL, D_MODEL), D_MODEL ** -0.5),
        "b_ple_gate": nrm(ks[21], (DEPTH, D_MODEL), 0.01),
    }


def reference(x_prompt, x_sample, p_prompt, p_sample, state_hgrn, state_ret, lb_logits, w_in, a_norm_g, b_norm_g, b_norm_b, w_out, ln1_g, ln1_b, w_ffn_gate, w_ffn_up, w_ffn_down, ln2_g, ln2_b, w_ple_proj, w_ple_gate, b_ple_gate):
    f32 = jnp.float32
    lower_bounds = jnp.cumsum(jax.nn.softmax(lb_logits.astype(f32), axis=0), axis=0)
    n_prompt, len_prompt = x_prompt.shape[0], x_prompt.shape[1]
    len_sample = x_sample.shape[1]
    pos_prompt = jnp.arange(len_prompt, dtype=f32)
    pos_sample = PAST_LEN + jnp.arange(len_sample, dtype=f32)
    zero_a = jnp.zeros((n_prompt, A_HEADS, A_HEAD_DIM, A_HEAD_DIM), state_hgrn.dtype)
    zero_b = jnp.zeros((n_prompt, B_HEADS, B_HEAD_DIM, B_HEAD_DIM), state_ret.dtype)
    hp, hs = x_prompt, x_sample
    sa_p, sb_p, sa_s, sb_s = [], [], [], []
    for i in range(DEPTH):
        lw = (w_in[i], a_norm_g[i], b_norm_g[i], b_norm_b[i], w_out[i], ln1_g[i], ln1_b[i], w_ffn_gate[i], w_ffn_up[i], w_ffn_down[i], ln2_g[i], ln2_b[i], w_ple_proj[i], w_ple_gate[i], b_ple_gate[i])
        hp, a_p, b_p = decoder_layer(hp, p_prompt[i], pos_prompt, lower_bounds[i], zero_a, zero_b, *lw)
        hs, a_s, b_s = decoder_layer(hs, p_sample[i], pos_sample, lower_bounds[i], state_hgrn[i], state_ret[i], *lw)
        sa_p.append(a_p)
        sb_p.append(b_p)
        sa_s.append(a_s)
        sb_s.append(b_s)
    return (hp, hs, jnp.stack(sa_p), jnp.stack(sb_p), jnp.stack(sa_s), jnp.stack(sb_s))
```

```python
import contextlib
import numpy as np
import concourse.bass as bass
import concourse.mybir as mybir
from concourse.bass_utils import run_bass_kernel_spmd

F32 = mybir.dt.float32
BF16 = mybir.dt.bfloat16
AF = mybir.ActivationFunctionType
ALU = mybir.AluOpType

NCORES = 8
D = 1024
DFF = 2816
NT = 17
NTOK = NT * 128
ALPHA = 2.0 ** 0.25
EPS = 1e-5
GAM = [1.0 - 2.0 ** (-5.0 - h) for h in range(4)]

O_ID = 0
O_TRI = {"P": 128, "S": 384}
O_TRIU = {"P": 256, "S": 512}
O_IND = {"P": 640, "S": 656}
O_MB = {"P": 672, "S": 1184}
O_GQ = {"P": 1696, "S": 2208}
O_KS = {"P": 2720, "S": 2724}
TABW = 2728


def host_tables():
    tab = np.zeros((128, TABW), np.float64)
    tab[:, O_ID:O_ID + 128] = np.eye(128)
    s = np.arange(128)
    for kind, C in (("P", 128), ("S", 8)):
        loc = s % C
        blk = s // C
        same = blk[:, None] == blk[None, :]
        tri = same & (s[:, None] <= s[None, :])
        triu = same & (s[:, None] > s[None, :])
        tab[:, O_TRI[kind]:O_TRI[kind] + 128] = tri
        tab[:, O_TRIU[kind]:O_TRIU[kind] + 128] = triu
        ind = np.zeros((128, 16))
        ind[s, blk] = 1.0
        tab[:, O_IND[kind]:O_IND[kind] + 16] = ind
        for h in range(4):
            g = GAM[h]
            mb = tri * (g ** (-(loc[:, None] + 1.0))) * (128.0 ** -0.5)
            tab[:, O_MB[kind] + h * 128:O_MB[kind] + (h + 1) * 128] = mb
            tab[:, O_GQ[kind] + h * 128:O_GQ[kind] + (h + 1) * 128] = (g ** (loc + 1.0))[None, :]
            tab[:, O_KS[kind] + h] = (g ** (C - 1.0 - loc)) * (128.0 ** -0.5)
    csn = np.zeros((NT, 128, 256), np.float64)
    inv = 10000.0 ** (-np.arange(64, dtype=np.float64) / 64.0)
    for i in range(NT):
        if i < 16:
            pos = i * 128 + np.arange(128, dtype=np.float64)
        else:
            pos = 16384.0 + (np.arange(128) % 8).astype(np.float64)
        ang = pos[:, None] * inv[None, :]
        c, sn = np.cos(ang), np.sin(ang)
        csn[i, :, 0:64] = c
        csn[i, :, 64:128] = c
        csn[i, :, 128:192] = -sn
        csn[i, :, 192:256] = sn
    return tab.astype(np.float32), csn.astype(np.float32)


class Sched:
    ENG = ("pe", "act", "dve", "pool", "sp")

    def __init__(self, nc, sems, free_chans):
        self.nc = nc
        self.sems = sems
        self.free_chans = list(free_chans)
        self.q = {e: [] for e in self.ENG}
        self.cnt = {e: 0 for e in self.ENG}
        self.waited = {e: {} for e in self.ENG}
        self.lastw = {}
        self.readers = {}
        self.nwaits = 0
        self.nops = {e: 0 for e in self.ENG}
        self.know = {}

    def chan(self, name):
        if name not in self.sems:
            self.sems[name] = self.free_chans.pop()
            self.cnt[name] = 0
        return name

    def _deps(self, eng, reads, writes):
        d = {}

        def add(tok):
            if tok is None:
                return
            sk, v = tok
            if d.get(sk, 0) < v:
                d[sk] = v
        for r in reads:
            add(self.lastw.get(r))
        for w in writes:
            add(self.lastw.get(w))
            for t in self.readers.get(w, ()):
                add(t)
        kn = self.waited[eng]
        waits = []
        for sk, v in sorted(d.items(), key=lambda kv: -len(self.know.get((kv[0], kv[1]), ()))):
            if kn.get(sk, 0) >= v:
                continue
            waits.append((sk, v))
            kn[sk] = v
            for k2, v2 in self.know.get((sk, v), {}).items():
                if kn.get(k2, 0) < v2:
                    kn[k2] = v2
        self.nwaits += len(waits)
        return waits

    def _finish(self, tok, reads, writes):
        for r in reads:
            self.readers.setdefault(r, []).append(tok)
        for w in writes:
            self.lastw[w] = tok
            self.readers[w] = []

    @staticmethod
    def _excl(reads, writes):
        ps = [r for r in reads if isinstance(r, tuple) and r[0] == "ps"]
        if ps:
            reads = [r for r in reads if not (isinstance(r, tuple) and r[0] == "ps")]
            writes = list(writes) + ps
        return reads, writes

    def op(self, eng, fn, reads=(), writes=()):
        reads, writes = self._excl(reads, writes)
        waits = self._deps(eng, reads, writes)
        self.cnt[eng] += 1
        self.nops[eng] += 1
        tok = (eng, self.cnt[eng])
        self.know[tok] = dict(self.waited[eng])
        sem = self.sems[eng]
        sems = self.sems

        def emit(e):
            for sk, v in waits:
                e.wait_ge(sems[sk], v)
            fn(e).then_inc(sem, 1)
        self.q[eng].append(emit)
        self._finish(tok, reads, writes)

    def dma(self, chan, fn, reads=(), writes=(), eng="sp"):
        self.chan(chan)
        waits = self._deps(eng, reads, writes)
        self.cnt[chan] += 16
        self.nops[eng] += 1
        tok = (chan, self.cnt[chan])
        self.know[tok] = dict(self.waited[eng])
        sem = self.sems[chan]
        sems = self.sems

        def emit(e):
            for sk, v in waits:
                e.wait_ge(sems[sk], v)
            fn(e).then_inc(sem, 16)
        self.q[eng].append(emit)
        self._finish(tok, reads, writes)

    def wait_all_dma(self, eng="sp"):
        waits = [(sk, v) for sk, v in self.cnt.items() if sk not in self.ENG and v > 0
                 and self.waited[eng].get(sk, 0) < v]
        for sk, v in waits:
            self.waited[eng][sk] = v
        sems = self.sems

        def emit(e):
            for sk, v in waits:
                e.wait_ge(sems[sk], v)
        self.q[eng].append(emit)

    def flush(self, block):
        qs = self.q

        @block.tensor
        def _(e):
            for f in qs["pe"]:
                f(e)

        @block.scalar
        def _(e):
            for f in qs["act"]:
                f(e)

        @block.vector
        def _(e):
            for f in qs["dve"]:
                f(e)

        @block.gpsimd
        def _(e):
            for f in qs["pool"]:
                f(e)

        @block.sync
        def _(e):
            for f in qs["sp"]:
                f(e)
        self.q = {e: [] for e in self.ENG}

    def after_barrier(self):
        for e in self.ENG:
            for sk, v in self.cnt.items():
                self.waited[e][sk] = v


def run_interleaved(gens):
    gens = [g for g in gens if g is not None]
    while gens:
        for g in list(gens):
            try:
                next(g)
            except StopIteration:
                gens.remove(g)


def run_pair(b, f):
    b_alive, f_alive, hold = True, f is not None, False
    while b_alive or f_alive:
        if b_alive and not (hold and f_alive):
            try:
                if next(b) == "TAIL":
                    hold = True
            except StopIteration:
                b_alive = False
        if f_alive:
            try:
                next(f)
            except StopIteration:
                f_alive = False


def build(debug=False):
    nc = bass.Bass("TRN2", target_bir_lowering=False, dynamic_dma_scratch_size=512)

    def din(name, shape, dt=F32):
        return nc.dram_tensor(name, list(shape), dt, kind="ExternalInput").ap()

    def dout(name, shape):
        return nc.dram_tensor(name, list(shape), F32, kind="ExternalOutput").ap()

    xp = din("xp", [2048, D]); xs = din("xs", [128, D])
    pp_d = din("pp", [2048, 256]); ps_d = din("ps", [128, 256])
    sab = din("sab", [16, 128, 1024])
    lbl = din("lbl", [2, 512])
    w_in = din("w_in", [D, 4096]); w_out = din("w_out", [D, D])
    ang = din("ang", [512]); bng = din("bng", [512]); bnb = din("bnb", [512])
    ln1g = din("ln1g", [D]); ln1b = din("ln1b", [D]); ln2g = din("ln2g", [D]); ln2b = din("ln2b", [D])
    wg = din("wg", [D, DFF]); wu = din("wu", [D, DFF]); wd = din("wd", [DFF, D])
    wpp = din("wpp", [256, D]); wpg = din("wpg", [D, D]); bpg = din("bpg", [D])
    tab_d = din("tab", [128, TABW]); csn_d = din("csn", [NT, 128, 256])

    yp = dout("yp", [2048, D]); ys = dout("ys", [128, D])
    nsa_p = dout("nsa_p", [4, 128, 128]); nsb_p = dout("nsb_p", [4, 128, 128])
    nsab_s = dout("nsab_s", [16, 128, 1024])
    dbg = {}
    if debug:
        for nm, shp in (("d_h", [NT, 128, D]), ("d_r2", [NT, 128, D])):
            dbg[nm] = dout(nm, shp)

    def x_ap(i):
        return xp[i * 128:(i + 1) * 128, :] if i < 16 else xs

    def y_ap(i):
        return yp[i * 128:(i + 1) * 128, :] if i < 16 else ys

    def p_ap(i):
        return pp_d[i * 128:(i + 1) * 128, :] if i < 16 else ps_d

    top = contextlib.ExitStack()
    with top:
        sems = {}
        for n in Sched.ENG:
            sems[n] = top.enter_context(nc.semaphore(n))
        free_chans = [top.enter_context(nc.semaphore("ch%d" % i)) for i in range(60)]
        S = Sched(nc, sems, free_chans)

        def sbt(stack, name, shape, dt=F32):
            return stack.enter_context(nc.sbuf_tensor(name, list(shape), dt))

        psb = [top.enter_context(nc.psum_tensor("psb%d" % i, [128, 512], F32)) for i in range(8)]
        pinned = set()
        rr = [0]

        def nb():
            while True:
                i = rr[0] % 8
                rr[0] += 1
                if i not in pinned:
                    return i

        def PK(i):
            return ("ps", i)

        def v4(ap):
            return ap.rearrange("p (h d) -> p h d", h=4)

        hT = sbt(top, "hT", [128, 8, NTOK], BF16)
        stage = [sbt(top, "stage%d" % i, [128, 1024]) for i in range(2)]
        tab = sbt(top, "tabs", [128, TABW])
        identb = sbt(top, "identb", [128, 128], BF16)
        xsl = [sbt(top, "xsl%d" % i, [128, D]) for i in range(2)]
        cm05 = sbt(top, "cm05", [128, 8])
        one_c = sbt(top, "one_c", [128, 1])
        w16 = sbt(top, "w16", [128, 8, D], BF16)
        wbig = sbt(top, "wbig", [128, 32768], BF16)
        w_in_bf = wbig[:].rearrange("p (k c) -> p k c", k=8)
        wgs = [wbig[:, s_ * 12288:s_ * 12288 + 4096].rearrange("p (k c) -> p k c", k=8) for s_ in range(2)]
        wus = [wbig[:, s_ * 12288 + 4096:s_ * 12288 + 8192].rearrange("p (k c) -> p k c", k=8) for s_ in range(2)]
        wds = [wbig[:, s_ * 12288 + 8192:s_ * 12288 + 12288].rearrange("p (k c) -> p k c", k=4) for s_ in range(2)]
        r_t = [wbig[:, 24576 + i * 2048:24576 + (i + 1) * 2048].bitcast(F32) for i in range(2)]
        actb = [wbig[:, 28672 + i * 2048:28672 + (i + 1) * 2048].rearrange("p (k c) -> p k c", k=4) for i in range(2)]
        WIN_KEYS = [("win", cb) for cb in range(4)]
        SLOT0 = [("wg", 0), ("wu", 0), ("wd", 0)]
        identf = tab[:, O_ID:O_ID + 128]
        stg_i = [0]

        hTf = hT[:].rearrange("p a b -> p (a b)")
        NXS = 6
        xstage = [hTf[:, i * 2048:(i + 1) * 2048].bitcast(F32) for i in range(NXS)]

        def load_cast(dst_ap, src_ap, shape3, wkey, eng="pool", wide=False, wkeys_extra=()):
            nsl = 2 + NXS if wide else 2
            sl = stg_i[0] % nsl
            stg_i[0] += 1
            a, b = shape3
            if sl < 2:
                st_ap = stage[sl][:, 0:a * b]
                keys = [("stage", sl)]
            else:
                st_ap = xstage[sl - 2][:, 0:a * b]
                keys = [("stage", sl), "hTalias"]
            st_v = st_ap.rearrange("p (a b) -> p a b", a=a) if a > 1 else st_ap
            S.dma("stg%d" % sl, lambda e: e.dma_start(out=st_v, in_=src_ap), writes=keys)
            if eng == "act":
                S.op("act", lambda e: e.activation(out=dst_ap, in_=st_v, func=AF.Copy), reads=[("stage", sl)],
                     writes=[wkey] + list(wkeys_extra))
            else:
                S.op(eng, lambda e: e.tensor_copy(out=dst_ap, in_=st_v), reads=[("stage", sl)], writes=[wkey] + list(wkeys_extra))

        GRP = [(0, 4), (512, 4), (2560, 2), (1024, 4), (1536, 4), (2048, 4)]
        NG = 6
        cast_rr = [0]

        def gen_load_ffn_group(G, engs, extra=()):
            slw = G % 2
            c0, nch = GRP[G]
            w = nch * 128
            ex = list(extra)

            def eng():
                cast_rr[0] += 1
                return engs[cast_rr[0] % len(engs)]
            for kk in range(0, 8, 2):
                load_cast(wgs[slw][:, kk:kk + 2, 0:w],
                          wg[kk * 128:(kk + 2) * 128, c0:c0 + w].rearrange("(k p) c -> p k c", p=128),
                          (2, w), ("wg", slw), eng(), wkeys_extra=ex)
                yield
            for kk in range(0, 8, 2):
                load_cast(wus[slw][:, kk:kk + 2, 0:w],
                          wu[kk * 128:(kk + 2) * 128, c0:c0 + w].rearrange("(k p) c -> p k c", p=128),
                          (2, w), ("wu", slw), eng(), wkeys_extra=ex)
                yield
            for c in range(nch):
                load_cast(wds[slw][:, c, :], wd[c0 + c * 128:c0 + (c + 1) * 128, :], (1, 1024), ("wd", slw), eng(), wkeys_extra=ex)
                yield

        def load_ffn_group(G, engs, extra=()):
            for _ in gen_load_ffn_group(G, engs, extra):
                pass

        a1 = contextlib.ExitStack()
        with a1:
            oml = sbt(a1, "oml", [128, 512])
            l0 = sbt(a1, "l0", [128, 512]); l1 = sbt(a1, "l1", [128, 512])
            bnb_bc = sbt(a1, "bnb_bc", [128, 512])
            csn = [sbt(a1, "csn%d" % i, [128, 256]) for i in range(2)]
            xT = sbt(a1, "xT", [128, 8, 128], BF16)
            Tbig = sbt(a1, "Tbig", [128, 13 * 512])
            T = [Tbig[:, i * 512:(i + 1) * 512] for i in range(13)]
            kte_all = [sbt(a1, "kte_all%d" % i, [128, 1024], BF16) for i in range(2)]
            v_all = [sbt(a1, "v_all%d" % i, [128, 1024], BF16) for i in range(2)]
            qd = [sbt(a1, "qd%d" % i, [128, 512], BF16) for i in range(2)]
            kd = [sbt(a1, "kd%d" % i, [128, 512], BF16) for i in range(2)]
            qr = [sbt(a1, "qr%d" % i, [128, 512], BF16) for i in range(2)]
            kr = [sbt(a1, "kr%d" % i, [128, 512], BF16) for i in range(2)]
            GA = [sbt(a1, "GA%d" % i, [128, 512]) for i in range(2)]
            GB = [sbt(a1, "GB%d" % i, [128, 512]) for i in range(2)]
            HB = [sbt(a1, "HB%d" % i, [128, 512]) for i in range(2)]
            decA = [sbt(a1, "decA%d" % i, [128, 64]) for i in range(2)]
            qkT = {"A": sbt(a1, "qkT_A", [128, 8, 128], BF16), "B": sbt(a1, "qkT_B", [128, 8, 128], BF16)}
            scm = {"A": sbt(a1, "scm_A", [128, 4, 128], BF16), "B": sbt(a1, "scm_B", [128, 4, 128], BF16)}
            mixin = sbt(a1, "mixin", [128, D], BF16)
            Sm = sbt(a1, "Sm", [128, 8, 128]); Sbf = sbt(a1, "Sbf", [128, 8, 128], BF16)
            st6 = sbt(a1, "st6", [128, 8, 6]); mv = sbt(a1, "mv", [128, 8, 2])
            sm1 = sbt(a1, "sm1", [128, 8]); rstd = sbt(a1, "rstd", [128, 8])
            S0 = [Tbig[:, (2 + 2 * i) * 512:(4 + 2 * i) * 512].rearrange("p (h v) -> p h v", h=8) for i in range(3)]
            S0K = [["T%d" % (2 + 2 * i), "T%d" % (3 + 2 * i)] for i in range(3)]
            S0bf = [Tbig[:, (8 + i) * 512:(9 + i) * 512].bitcast(BF16).rearrange("p (h v) -> p h v", h=8) for i in range(2)]
            S0bfK = [["T8"], ["T9"]]
            ktem = [Tbig[:, (10 + i) * 512:(11 + i) * 512].bitcast(BF16) for i in range(2)]
            ktemK = [["T10"], ["T11"]]
            zerob = sbt(a1, "zerob", [128, 128], BF16)

            with nc.Block(no_gpsimd_drain=True) as block:
                S.dma("tab", lambda e: e.dma_start(out=tab[:], in_=tab_d), writes=["tab"])
                S.dma("l0", lambda e: e.dma_start(out=l0[:], in_=lbl[0, :].partition_broadcast(128)), writes=["l0"])
                S.dma("l1", lambda e: e.dma_start(out=l1[:], in_=lbl[1, :].partition_broadcast(128)), writes=["l1"])
                S.dma("bnb", lambda e: e.dma_start(out=bnb_bc[:], in_=bnb.partition_broadcast(128)), writes=["bnb_bc"])
                S.op("pool", lambda e: e.tensor_copy(out=identb[:], in_=identf), reads=["tab"], writes=["identb"])
                S.op("pool", lambda e: e.memset(cm05[:], -0.5), writes=["cm05"])
                S.op("pool", lambda e: e.memset(one_c[:], 1.0), writes=["one_c"])
                S.op("pool", lambda e: e.memset(zerob[:], 0.0), writes=["zerob"])
                S.op("pool", lambda e: e.memset(Sm[:], 0.0), writes=["SmA", "SmB"])
                S.op("pool", lambda e: e.memset(Sbf[:], 0.0), writes=["SbfA", "SbfB"])
                S.op("pool", lambda e: e.tensor_tensor(out=l0[:], in0=l0[:], in1=l1[:], op=ALU.subtract),
                     reads=["l0", "l1"], writes=["l0"])
                S.op("act", lambda e: e.activation(out=oml[:], in_=l0[:], func=AF.Sigmoid, scale=-1.0),
                     reads=["l0"], writes=["oml"])
                ang_bc, bng_bc = l0, l1
                S.dma("l0", lambda e: e.dma_start(out=ang_bc[:], in_=ang.partition_broadcast(128)), writes=["l0"])
                S.dma("l1", lambda e: e.dma_start(out=bng_bc[:], in_=bng.partition_broadcast(128)), writes=["l1"])

                def load_tile_inputs(i):
                    sl = i % 2
                    S.dma("x%d" % sl, lambda e: e.dma_start(out=xsl[sl][:], in_=x_ap(i)), writes=[("xsl", sl)])
                    S.dma("csn%d" % sl, lambda e: e.dma_start(out=csn[sl][:], in_=csn_d[i]), writes=[("csn", sl)])

                load_tile_inputs(0)

                win_emitted = {}

                def win_loader():
                    ci = 0
                    for cb in range(4):
                        for k in range(8):
                            load_cast(w_in_bf[:, k, cb * 1024:(cb + 1) * 1024],
                                      w_in[k * 128:(k + 1) * 128, cb * 1024:(cb + 1) * 1024], (1, 1024), ("win", cb),
                                      eng=("act" if ci % 2 == 0 else "dve"), wide=True)
                            ci += 1
                            win_emitted[cb] = win_emitted.get(cb, 0) + 1
                            if cb > 0 and k % 4 == 3:
                                yield
                    stg_i[0] = 0

                def tabs_for(kind):
                    return dict(
                        TRI=tab[:, O_TRI[kind]:O_TRI[kind] + 128], TRIU=tab[:, O_TRIU[kind]:O_TRIU[kind] + 128],
                        IND=tab[:, O_IND[kind]:O_IND[kind] + 16], MB=tab[:, O_MB[kind]:O_MB[kind] + 512],
                        GQ=tab[:, O_GQ[kind]:O_GQ[kind] + 512], KS=tab[:, O_KS[kind]:O_KS[kind] + 4])

                def front(i):
                    kind = "S" if i == 16 else "P"
                    tb_ = tabs_for(kind)
                    TRI, TRIU, IND, KS = tb_["TRI"], tb_["TRIU"], tb_["IND"], tb_["KS"]
                    sl = i % 2
                    par = i % 2
                    P = str(par)
                    xt = xsl[sl]
                    if i + 1 < NT:
                        load_tile_inputs(i + 1)
                    for half in range(2):
                        bk = nb()

                        def f(e, bk=bk, half=half):
                            r = None
                            for q in range(4):
                                kq = half * 4 + q
                                r = e.transpose(psb[bk][:, q * 128:(q + 1) * 128], xt[:, kq * 128:(kq + 1) * 128], identf)
                            return r
                        S.op("pe", f, reads=[("xsl", sl), "tab"], writes=[PK(bk)])
                        dst = xT[:, half * 4:(half + 1) * 4, :].rearrange("p a b -> p (a b)")
                        if half == 0:
                            S.op("act", lambda e, bk=bk, dst=dst: e.activation(out=dst, in_=psb[bk][:], func=AF.Copy),
                                 reads=[PK(bk)], writes=[("xT", 0)])
                        else:
                            S.op("act", lambda e, bk=bk, dst=dst: e.activation(out=dst, in_=psb[bk][:], func=AF.Copy),
                                 reads=[PK(bk)], writes=[("xT", 1)])
                        yield

                    def proj(b):
                        assert win_emitted.get(b // 2, 0) == 8, "w_in block not loaded before use"
                        bk = nb()

                        def f(e):
                            r = None
                            for k in range(8):
                                r = e.matmul(psb[bk][:], lhsT=xT[:, k, :], rhs=w_in_bf[:, k, b * 512:(b + 1) * 512],
                                             start=(k == 0), stop=(k == 7))
                            return r
                        S.op("pe", f, reads=[("xT", 0), ("xT", 1), ("win", b // 2)], writes=[PK(bk)])
                        return bk

                    cs = csn[sl][:, 0:128]
                    sn = csn[sl][:, 128:256]

                    def rope(b, t_a, t_b, t_c, outbf, outkey):
                        bk = proj(b)
                        ka, kb_, kc = "T%d" % t_a, "T%d" % t_b, "T%d" % t_c
                        S.op("dve", lambda e: e.tensor_tensor(out=v4(T[t_a][:]), in0=v4(psb[bk][:]),
                                                              in1=cs[:, None, :].to_broadcast([128, 4, 128]), op=ALU.mult),
                             reads=[PK(bk), ("csn", sl)], writes=[ka])
                        S.op("act", lambda e: e.activation(out=T[t_b][:], in_=psb[bk][:], func=AF.Copy),
                             reads=[PK(bk)], writes=[kb_])
                        S.op("dve", lambda e: e.tensor_tensor(out=v4(T[t_c][:])[:, :, 0:64], in0=v4(T[t_b][:])[:, :, 64:128],
                                                              in1=sn[:, None, 0:64].to_broadcast([128, 4, 64]), op=ALU.mult),
                             reads=[kb_, ("csn", sl)], writes=[kc])
                        S.op("dve", lambda e: e.tensor_tensor(out=v4(T[t_c][:])[:, :, 64:128], in0=v4(T[t_b][:])[:, :, 0:64],
                                                              in1=sn[:, None, 64:128].to_broadcast([128, 4, 64]), op=ALU.mult),
                             reads=[kb_, ("csn", sl)], writes=[kc])
                        S.op("pool", lambda e: e.tensor_tensor(out=outbf[:], in0=T[t_a][:], in1=T[t_c][:], op=ALU.add),
                             reads=[ka, kc], writes=[outkey])

                    bk = proj(1)
                    S.op("act", lambda e, bk=bk: e.activation(out=T[1][:], in_=psb[bk][:], func=AF.Sigmoid, scale=-1.0),
                         reads=[PK(bk)], writes=["T1"])
                    S.op("dve", lambda e: e.tensor_tensor(out=T[4][:], in0=T[1][:], in1=oml[:], op=ALU.mult),
                         reads=["T1", "oml"], writes=["T4"])
                    S.op("act", lambda e: e.activation(out=T[5][:], in_=T[4][:], func=AF.Ln, scale=-1.0, bias=one_c[:]),
                         reads=["T4", "one_c"], writes=["T5"])
                    yield
                    rope(5, 10, 11, 12, kr[par], "kr" + P)
                    for h in range(4):
                        S.op("dve", lambda e, h=h: e.tensor_scalar(
                            out=kte_all[par][:, 512 + h * 128:512 + (h + 1) * 128], in0=kr[par][:, h * 128:(h + 1) * 128],
                            scalar1=KS[:, h:h + 1], scalar2=None, op0=ALU.mult),
                            reads=["kr" + P, "tab"], writes=["kteB" + P])
                    yield
                    bk6 = proj(6)
                    S.op("act", lambda e: e.activation(out=v_all[par][:, 512:1024], in_=psb[bk6][:], func=AF.Copy),
                         reads=[PK(bk6)], writes=["vB" + P])
                    yield
                    bk0 = proj(0)
                    S.op("act", lambda e: e.activation(out=T[0][:], in_=psb[bk0][:], func=AF.Silu),
                         reads=[PK(bk0)], writes=["T0"])
                    yield
                    bkb, bks, bkd = nb(), nb(), nb()
                    S.op("pe", lambda e: e.matmul(psb[bkb][:], lhsT=TRI, rhs=T[5][:], start=True, stop=True),
                         reads=["T5", "tab"], writes=[PK(bkb)])
                    S.op("act", lambda e: e.activation(out=T[6][:], in_=psb[bkb][:], func=AF.Exp),
                         reads=[PK(bkb)], writes=["T6"])
                    S.op("act", lambda e: e.activation(out=T[7][:], in_=psb[bkb][:], func=AF.Exp, scale=-1.0),
                         reads=[PK(bkb)], writes=["T7"])
                    yield
                    S.op("pe", lambda e: e.matmul(psb[bks][:], lhsT=TRIU, rhs=T[5][:], start=True, stop=True),
                         reads=["T5", "tab"], writes=[PK(bks)])
                    S.op("act", lambda e: e.activation(out=T[8][:], in_=psb[bks][:], func=AF.Exp),
                         reads=[PK(bks)], writes=["T8"])

                    def fdec(e):
                        r = None
                        for h in range(4):
                            r = e.matmul(psb[bkd][:, h * 16:(h + 1) * 16], lhsT=T[5][:, h * 128:(h + 1) * 128], rhs=IND,
                                         start=True, stop=True)
                        return r
                    S.op("pe", fdec, reads=["T5", "tab"], writes=[PK(bkd)])
                    S.op("act", lambda e: e.activation(out=decA[par][:], in_=psb[bkd][:, 0:64], func=AF.Exp),
                         reads=[PK(bkd)], writes=["decA" + P])
                    S.op("dve", lambda e: e.tensor_tensor(out=qd[par][:], in0=T[0][:], in1=T[6][:], op=ALU.mult),
                         reads=["T0", "T6"], writes=["qd" + P])
                    S.op("dve", lambda e: e.tensor_tensor(out=kd[par][:], in0=T[4][:], in1=T[7][:], op=ALU.mult),
                         reads=["T4", "T7"], writes=["kd" + P])
                    S.op("dve", lambda e: e.tensor_tensor(out=kte_all[par][:, 0:512], in0=T[4][:], in1=T[8][:], op=ALU.mult),
                         reads=["T4", "T8"], writes=["kteA" + P])
                    yield
                    bk2 = proj(2)
                    S.op("act", lambda e: e.activation(out=v_all[par][:, 0:512], in_=psb[bk2][:], func=AF.Copy),
                         reads=[PK(bk2)], writes=["vA" + P])
                    yield
                    rope(4, 2, 3, 9, qr[par], "qr" + P)
                    yield
                    bk3 = proj(3)
                    S.op("act", lambda e: e.activation(out=GA[par][:], in_=psb[bk3][:], func=AF.Silu),
                         reads=[PK(bk3)], writes=["GA" + P])
                    S.op("pool", lambda e: e.tensor_tensor(out=GA[par][:], in0=GA[par][:], in1=ang_bc[:], op=ALU.mult),
                         reads=["GA" + P, "l0"], writes=["GA" + P])
                    yield

                    bk7 = proj(7)
                    S.op("act", lambda e: e.activation(out=GB[par][:], in_=psb[bk7][:], func=AF.Silu),
                         reads=[PK(bk7)], writes=["GB" + P])
                    S.op("pool", lambda e: e.tensor_tensor(out=HB[par][:], in0=GB[par][:], in1=bnb_bc[:], op=ALU.mult),
                         reads=["GB" + P, "bnb_bc"], writes=["HB" + P])
                    S.op("pool", lambda e: e.tensor_tensor(out=GB[par][:], in0=GB[par][:], in1=bng_bc[:], op=ALU.mult),
                         reads=["GB" + P, "l1"], writes=["GB" + P])
                    yield

                def back(i):
                    kind = "S" if i == 16 else "P"
                    tb_ = tabs_for(kind)
                    TRI, IND, MB, GQ = tb_["TRI"], tb_["IND"], tb_["MB"], tb_["GQ"]
                    par = i % 2
                    P = str(par)
                    CB = 8 if kind == "S" else 128
                    ktea, va = kte_all[par], v_all[par]
                    pins = []

                    def group(g):
                        go = 0 if g == "A" else 4
                        fo = 0 if g == "A" else 512
                        qsrc, ksrc = (qd[par], kd[par]) if g == "A" else (qr[par], kr[par])
                        qkey, kkey = ("qd" + P, "kd" + P) if g == "A" else ("qr" + P, "kr" + P)
                        qk = qkT[g]
                        if kind == "P":
                            bkv = nb()

                            def fkv(e):
                                r = None
                                for h in range(4):
                                    r = e.matmul(psb[bkv][:, h * 128:(h + 1) * 128],
                                                 lhsT=ktea[:, fo + h * 128:fo + (h + 1) * 128],
                                                 rhs=va[:, fo + h * 128:fo + (h + 1) * 128], start=True, stop=True)
                                return r
                            S.op("pe", fkv, reads=["kte" + g + P, "v" + g + P], writes=[PK(bkv)])
                            for h in range(4):
                                sc = decA[par][:, h * 16:h * 16 + 1] if g == "A" else float(GAM[h] ** CB)
                                S.op("dve", lambda e, h=h, sc=sc: e.scalar_tensor_tensor(
                                    out=Sm[:, go + h, :], in0=Sm[:, go + h, :], scalar=sc,
                                    in1=psb[bkv][:, h * 128:(h + 1) * 128], op0=ALU.mult, op1=ALU.add),
                                    reads=[PK(bkv), "Sm" + g, "decA" + P], writes=["Sm" + g])
                            yield
                        bk = nb()
                        pb = psb[bk][:].bitcast(BF16)

                        def ftr(e):
                            r = None
                            for h in range(4):
                                r = e.transpose(pb[:, h * 128:(h + 1) * 128], qsrc[:, h * 128:(h + 1) * 128], identb[:])
                            for h in range(4):
                                r = e.transpose(pb[:, (4 + h) * 128:(5 + h) * 128], ksrc[:, h * 128:(h + 1) * 128], identb[:])
                            return r
                        S.op("pe", ftr, reads=[qkey, kkey, "identb"], writes=[PK(bk)])
                        qkf = qk[:].rearrange("p a b -> p (a b)")
                        if g == "A":
                            S.op("act", lambda e: e.activation(out=qkf, in_=pb, func=AF.Copy), reads=[PK(bk)], writes=["qkT" + g])
                        else:
                            S.op("dve", lambda e: e.tensor_tensor(out=qkf[:, 0:512], in0=pb[:, 0:512], in1=GQ, op=ALU.mult),
                                 reads=[PK(bk), "tab"], writes=["qkT" + g])
                            S.op("act", lambda e: e.activation(out=qkf[:, 512:1024], in_=pb[:, 512:1024], func=AF.Copy),
                                 reads=[PK(bk)], writes=["qkT" + g + "k"])
                        qk_reads = ["qkT" + g] + (["qkT" + g + "k"] if g == "B" else [])
                        yield
                        bsc = nb()

                        def fsc(e):
                            r = None
                            for h in range(4):
                                r = e.matmul(psb[bsc][:, h * 128:(h + 1) * 128], lhsT=qk[:, 4 + h, :], rhs=qk[:, h, :],
                                             start=True, stop=True)
                            return r
                        S.op("pe", fsc, reads=qk_reads, writes=[PK(bsc)])
                        scf = scm[g][:].rearrange("p a b -> p (a b)")
                        if g == "A":
                            S.op("dve", lambda e: e.tensor_tensor(out=scm[g][:], in0=v4(psb[bsc][:]),
                                                                  in1=TRI[:, None, :].to_broadcast([128, 4, 128]), op=ALU.mult),
                                 reads=[PK(bsc), "tab"], writes=["scm" + g])
                        else:
                            S.op("dve", lambda e: e.tensor_tensor(out=scf, in0=psb[bsc][:], in1=MB, op=ALU.mult),
                                 reads=[PK(bsc), "tab"], writes=["scm" + g])
                        yield
                        bo = nb()
                        pinned.add(bo)
                        pins.append(bo)
                        bos[g] = bo

                        def fo_(e):
                            r = None
                            if kind == "S":
                                e.matmul(psb[bo][:], lhsT=zerob[:], rhs=va[:, fo:fo + 512], start=True, stop=False,
                                         skip_group_check=True)
                            for h in range(4):
                                r = e.matmul(psb[bo][:, h * 128:(h + 1) * 128], lhsT=scm[g][:, h, :],
                                             rhs=va[:, fo + h * 128:fo + (h + 1) * 128], start=(kind == "P"), stop=False,
                                             skip_group_check=(kind == "S"))
                                if kind == "P":
                                    r = e.matmul(psb[bo][:, h * 128:(h + 1) * 128], lhsT=qk[:, h, :], rhs=Sbf[:, go + h, :],
                                                 start=False, stop=True)
                            return r
                        S.op("pe", fo_, reads=["scm" + g, "v" + g + P, "Sbf" + g, "zerob"] + qk_reads, writes=[PK(bo)])
                        if kind == "P":
                            S.op("act", lambda e: e.activation(out=Sbf[:, go:go + 4, :], in_=Sm[:, go:go + 4, :], func=AF.Copy),
                                 reads=["Sm" + g], writes=["Sbf" + g])
                            norm_and_gate(g)
                        yield
                    def norm_and_gate(g):
                        go = 0 if g == "A" else 4
                        bo = bos[g]
                        for h in range(4):
                            S.op("dve", lambda e, h=h: e.bn_stats(out=st6[:, go + h, :], in_=psb[bo][:, h * 128:(h + 1) * 128]),
                                 reads=[PK(bo)], writes=["st6" + g])
                        for h in range(4):
                            S.op("dve", lambda e, h=h: e.bn_aggr(out=mv[:, go + h, :], in_=st6[:, go + h, :]),
                                 reads=["st6" + g], writes=["mv" + g])
                        mean = mv[:, go:go + 4, 0]
                        var = mv[:, go:go + 4, 1]
                        s1 = sm1[:, go:go + 4]
                        rs = rstd[:, go:go + 4]
                        if g == "A":
                            S.op("dve", lambda e: e.tensor_tensor(out=s1, in0=mean, in1=mean, op=ALU.mult),
                                 reads=["mv" + g], writes=["sm1" + g])
                            S.op("dve", lambda e: e.scalar_tensor_tensor(out=s1, in0=s1, scalar=EPS, in1=var,
                                                                         op0=ALU.add, op1=ALU.add),
                                 reads=["mv" + g, "sm1" + g], writes=["sm1" + g])
                        else:
                            S.op("dve", lambda e: e.tensor_scalar(out=s1, in0=var, scalar1=EPS, scalar2=None, op0=ALU.add),
                                 reads=["mv" + g], writes=["sm1" + g])
                        S.op("pool", lambda e: e.tensor_tensor(out=rs, in0=s1, in1=cm05[:, 0:4], op=ALU.pow),
                             reads=["sm1" + g, "cm05"], writes=["rstd" + g])
                        if g == "A":
                            for h in range(4):
                                S.op("dve", lambda e, h=h: e.scalar_tensor_tensor(
                                    out=mixin[:, h * 128:(h + 1) * 128], in0=psb[bo][:, h * 128:(h + 1) * 128],
                                    scalar=rstd[:, h:h + 1], in1=GA[par][:, h * 128:(h + 1) * 128], op0=ALU.mult, op1=ALU.mult),
                                    reads=[PK(bo), "rstdA", "GA" + P], writes=["mixinA"])
                        else:
                            for h in range(4):
                                hs = slice(h * 128, (h + 1) * 128)
                                S.op("dve", lambda e, h=h, hs=hs: e.scalar_tensor_tensor(
                                    out=GB[par][:, hs], in0=psb[bo][:, hs], scalar=mv[:, 4 + h, 0:1], in1=GB[par][:, hs],
                                    op0=ALU.subtract, op1=ALU.mult),
                                    reads=[PK(bo), "mvB", "GB" + P], writes=["GB" + P])
                            for h in range(4):
                                hs = slice(h * 128, (h + 1) * 128)
                                S.op("dve", lambda e, h=h, hs=hs: e.scalar_tensor_tensor(
                                    out=mixin[:, 512 + h * 128:512 + (h + 1) * 128], in0=GB[par][:, hs],
                                    scalar=rstd[:, 4 + h:5 + h], in1=HB[par][:, hs], op0=ALU.mult, op1=ALU.add),
                                    reads=["GB" + P, "rstdB", "HB" + P], writes=["mixinB"])

                    bos = {}
                    alive = [group("A"), group("B")]
                    while alive:
                        for g_ in list(alive):
                            try:
                                next(g_)
                            except StopIteration:
                                alive.remove(g_)
                        yield
                    boA, boB = bos["A"], bos["B"]

                    if kind == "S":
                        otA, otB = nb(), nb()
                        pinned.add(otA); pinned.add(otB)
                        pins.extend([otA, otB])
                        NS = 3

                        def load_state(j):
                            s3 = j % NS
                            S.dma("s0a%d" % s3, lambda e: e.dma_start(
                                out=S0[s3].rearrange("p a b -> p (a b)"), in_=sab[j]), writes=S0K[s3])
                        for j in range(NS):
                            load_state(j)
                        for j in range(16):
                            s3 = j % NS
                            s2 = j % 2
                            S.op("act", lambda e, s3=s3, s2=s2: e.activation(
                                out=S0bf[s2].rearrange("p a b -> p (a b)"), in_=S0[s3].rearrange("p a b -> p (a b)"),
                                func=AF.Copy),
                                reads=S0K[s3], writes=S0bfK[s2])

                            def fot(e, j=j, s2=s2):
                                r = None
                                for h in range(4):
                                    r = e.matmul(psb[otA][:, h * 128 + 8 * j:h * 128 + 8 * j + 8], lhsT=S0bf[s2][:, h, :],
                                                 rhs=qkT["A"][:, h, 8 * j:8 * j + 8], start=True, stop=True)
                                for h in range(4):
                                    r = e.matmul(psb[otB][:, h * 128 + 8 * j:h * 128 + 8 * j + 8], lhsT=S0bf[s2][:, 4 + h, :],
                                                 rhs=qkT["B"][:, h, 8 * j:8 * j + 8], start=True, stop=True)
                                return r
                            S.op("pe", fot, reads=S0bfK[s2] + ["qkTA", "qkTB", "qkTBk"], writes=[PK(otA), PK(otB)])
                            S.op("act", lambda e, j=j, s2=s2: e.activation(
                                out=ktem[s2], in_=ktea[:], func=AF.Copy, scale=IND[:, j:j + 1]),
                                reads=["kteA" + P, "kteB" + P, "tab"], writes=ktemK[s2])
                            kvA, kvB = nb(), nb()

                            def fkvs(e, s2=s2, kvA=kvA, kvB=kvB):
                                r = None
                                for h in range(8):
                                    bkx = kvA if h < 4 else kvB
                                    hh = h % 4
                                    r = e.matmul(psb[bkx][:, hh * 128:(hh + 1) * 128], lhsT=ktem[s2][:, h * 128:(h + 1) * 128],
                                                 rhs=va[:, h * 128:(h + 1) * 128], start=True, stop=True)
                                return r
                            S.op("pe", fkvs, reads=ktemK[s2] + ["vA" + P, "vB" + P], writes=[PK(kvA), PK(kvB)])
                            for h in range(8):
                                hh = h % 4
                                bkx = kvA if h < 4 else kvB
                                sc = decA[par][:, hh * 16 + j:hh * 16 + j + 1] if h < 4 else float(GAM[hh] ** 8)
                                S.op("dve", lambda e, h=h, hh=hh, bkx=bkx, sc=sc, s3=s3: e.scalar_tensor_tensor(
                                    out=S0[s3][:, h, :], in0=S0[s3][:, h, :], scalar=sc,
                                    in1=psb[bkx][:, hh * 128:(hh + 1) * 128], op0=ALU.mult, op1=ALU.add),
                                    reads=[PK(bkx), "decA" + P], writes=S0K[s3])
                            S.dma("s0a%d" % s3, lambda e, j=j, s3=s3: e.dma_start(
                                out=nsab_s[j], in_=S0[s3].rearrange("p a b -> p (a b)")), writes=S0K[s3])
                            if j + NS < 16:
                                load_state(j + NS)
                            yield
                        for g, ot, bo, osb, okey in (("A", otA, boA, T[0], "T0"), ("B", otB, boB, T[1], "T1")):
                            S.op("act", lambda e, ot=ot, osb=osb: e.activation(out=osb, in_=psb[ot][:], func=AF.Copy),
                                 reads=[PK(ot)], writes=[okey])

                            def facc(e, bo=bo, osb=osb):
                                r = None
                                for h in range(4):
                                    r = e.matmul(psb[bo][:, h * 128:(h + 1) * 128], lhsT=osb[:, h * 128:(h + 1) * 128],
                                                 rhs=identf, start=False, stop=True, skip_group_check=True)
                                return r
                            S.op("pe", facc, reads=[okey, "tab"], writes=[PK(bo)])
                        yield

                    if kind == "S":
                        norm_and_gate("A")
                        norm_and_gate("B")
                    for b_ in pins:
                        pinned.discard(b_)
                    yield "TAIL"


                    bk = nb()
                    pbm = psb[bk][:].bitcast(BF16)

                    def fmt(e):
                        r = None
                        for kq in range(8):
                            r = e.transpose(pbm[:, kq * 128:(kq + 1) * 128], mixin[:, kq * 128:(kq + 1) * 128], identb[:])
                        return r
                    S.op("pe", fmt, reads=["mixinA", "mixinB", "identb"], writes=[PK(bk)])
                    S.op("act", lambda e: e.activation(out=hT[:, :, i * 128:(i + 1) * 128],
                                                       in_=pbm.rearrange("p (a b) -> p a b", a=8), func=AF.Copy),
                         reads=[PK(bk)], writes=[("hT", i), "hTalias"])
                    yield
                    if i == 15:
                        S.dma("smA", lambda e: e.dma_start(out=nsa_p.rearrange("h k v -> k h v"), in_=Sm[:, 0:4, :]), reads=["SmA"])
                        S.dma("smB", lambda e: e.dma_start(out=nsb_p.rearrange("h k v -> k h v"), in_=Sm[:, 4:8, :]), reads=["SmB"])

                f0 = front(0)
                next(f0); next(f0)
                for _ in win_loader():
                    pass
                run_interleaved([f0])
                for i in range(NT):
                    if i == 2:
                        for k in range(8):
                            load_cast(w16[:, k, :], w_out[k * 128:(k + 1) * 128, :], (1, 1024), "w16", eng="pool")
                    if i + 1 < NT:
                        run_pair(back(i), front(i + 1))
                    else:
                        def ffn_prefetch():
                            for G in (0, 1):
                                yield from gen_load_ffn_group(G, ["pool"], extra=WIN_KEYS)
                        run_interleaved([back(i), ffn_prefetch()])
                S.wait_all_dma()
                S.flush(block)
            S.after_barrier()

        s2 = contextlib.ExitStack()
        with s2:
            R = sbt(s2, "R", [128, NT, D])
            st12 = [sbt(s2, "st12_%d" % i, [128, 2, 6]) for i in range(2)]
            mv2 = [sbt(s2, "mv2_%d" % i, [128, 2]) for i in range(2)]
            t2a = [sbt(s2, "t2a%d" % i, [128, 1]) for i in range(2)]
            rs2 = [sbt(s2, "rs2_%d" % i, [128, 1]) for i in range(2)]
            nm2 = [sbt(s2, "nm2_%d" % i, [128, 1]) for i in range(2)]
            g_bc = sbt(s2, "g_bc", [128, D]); b_bc = sbt(s2, "b_bc", [128, D])

            def ln_stats(src_ap, src_keys, par):
                P = str(par)
                for half in range(2):
                    S.op("dve", lambda e, half=half: e.bn_stats(out=st12[par][:, half, :], in_=src_ap[:, half * 512:(half + 1) * 512]),
                         reads=src_keys, writes=["st12" + P])
                S.op("dve", lambda e: e.bn_aggr(out=mv2[par][:], in_=st12[par][:].rearrange("p a b -> p (a b)")),
                     reads=["st12" + P], writes=["mv2" + P])
                S.op("dve", lambda e: e.tensor_scalar(out=t2a[par][:], in0=mv2[par][:, 1:2], scalar1=EPS, scalar2=None, op0=ALU.add),
                     reads=["mv2" + P], writes=["t2a" + P])
                S.op("pool", lambda e: e.tensor_tensor(out=rs2[par][:], in0=t2a[par][:], in1=cm05[:, 0:1], op=ALU.pow),
                     reads=["t2a" + P, "cm05"], writes=["rs2" + P])

            def ln_apply(src_ap, src_keys, dst_ap, dst_key, par, eng2="dve"):
                P = str(par)
                S.op("dve", lambda e: e.scalar_tensor_tensor(out=dst_ap, in0=src_ap, scalar=mv2[par][:, 0:1], in1=g_bc[:],
                                                             op0=ALU.subtract, op1=ALU.mult),
                     reads=src_keys + ["mv2" + P, "g_bc"], writes=[dst_key])
                S.op("dve", lambda e: e.scalar_tensor_tensor(out=dst_ap, in0=dst_ap, scalar=rs2[par][:, 0:1], in1=b_bc[:],
                                                             op0=ALU.mult, op1=ALU.add),
                     reads=[dst_key, "rs2" + P, "b_bc"], writes=[dst_key])

            def transposes_f32(src_ap, srckeys, n, dst_fn, dstkeys, scale=None):
                for c0 in range(0, n, 4):
                    cnt = min(4, n - c0)
                    bk = nb()

                    def f(e, bk=bk, c0=c0, cnt=cnt):
                        r = None
                        for q in range(cnt):
                            r = e.transpose(psb[bk][:, q * 128:(q + 1) * 128], src_ap[:, (c0 + q) * 128:(c0 + q + 1) * 128], identf)
                        return r
                    S.op("pe", f, reads=srckeys + ["tab"], writes=[PK(bk)])
                    dst = dst_fn(c0, c0 + cnt)
                    srcv = psb[bk][:, 0:cnt * 128].rearrange("p (a b) -> p a b", a=cnt)
                    sc = 1.0 if scale is None else float(scale)
                    if (c0 // 4) % 2 == 0:
                        S.op("act", lambda e, dst=dst, srcv=srcv, sc=sc: e.activation(out=dst, in_=srcv, func=AF.Copy, scale=sc),
                             reads=[PK(bk)], writes=[dstkeys[c0 // 4]])
                    else:
                        S.op("dve", lambda e, dst=dst, srcv=srcv, sc=sc: e.tensor_scalar(out=dst, in0=srcv, scalar1=sc, scalar2=None, op0=ALU.mult),
                             reads=[PK(bk)], writes=[dstkeys[c0 // 4]])

            sab = contextlib.ExitStack()
            with sab:
                sgt = [sbt(sab, "sgt%d" % i, [128, 512]) for i in range(2)]
                cast_rr = [0]

                with nc.Block(no_gpsimd_drain=True) as block:
                    S.dma("gbc", lambda e: e.dma_start(out=g_bc[:], in_=ln1g.partition_broadcast(128)), writes=["g_bc"])
                    S.dma("bbc", lambda e: e.dma_start(out=b_bc[:], in_=ln1b.partition_broadcast(128)), writes=["b_bc"])
                    S.dma("x0", lambda e: e.dma_start(out=xsl[0][:], in_=x_ap(0)), writes=[("xsl", 0)])
                    S.op("pool", lambda e: e.tensor_scalar(out=g_bc[:], in0=g_bc[:], scalar1=float(ALPHA), scalar2=None, op0=ALU.mult),
                         reads=["g_bc"], writes=["g_bc"])
                    S.op("pool", lambda e: e.tensor_scalar(out=b_bc[:], in0=b_bc[:], scalar1=float(ALPHA), scalar2=None, op0=ALU.mult),
                         reads=["b_bc"], writes=["b_bc"])

                    def a2_s12(i):
                        sl = i % 2
                        par = i % 2
                        if i + 1 < NT:
                            S.dma("x%d" % ((i + 1) % 2), lambda e: e.dma_start(out=xsl[(i + 1) % 2][:], in_=x_ap(i + 1)),
                                  writes=[("xsl", (i + 1) % 2)])
                        for half in range(2):
                            bk = nb()

                            def f(e, bk=bk, half=half):
                                r = None
                                for k in range(8):
                                    r = e.matmul(psb[bk][:], lhsT=hT[:, k, i * 128:(i + 1) * 128],
                                                 rhs=w16[:, k, half * 512:(half + 1) * 512], start=(k == 0), stop=(k == 7))
                                return r
                            S.op("pe", f, reads=[("hT", i), "w16"], writes=[PK(bk)])
                            S.op("dve", lambda e, bk=bk, half=half: e.scalar_tensor_tensor(
                                out=r_t[par][:, half * 512:(half + 1) * 512], in0=xsl[sl][:, half * 512:(half + 1) * 512],
                                scalar=float(ALPHA), in1=psb[bk][:], op0=ALU.mult, op1=ALU.add),
                                reads=[PK(bk), ("xsl", sl)], writes=[("r_t", par)])

                    def a2_stats(i):
                        par = i % 2
                        ln_stats(r_t[par][:], [("r_t", par)], par)

                    def a2_apply(i):
                        par = i % 2
                        ln_apply(r_t[par][:], [("r_t", par)], R[:, i, :], ("R", i), par)

                    def a2_block(tl):
                        for p0 in range(0, len(tl), 2):
                            pair = tl[p0:p0 + 2]
                            for t in pair:
                                a2_s12(t)
                            for t in pair:
                                a2_stats(t)
                            for t in pair:
                                a2_apply(t)

                    def a2_s3(i):
                        transposes_f32(R[:, i, :], [("R", i)], 8, lambda lo, hi: hT[:, lo:hi, i * 128:(i + 1) * 128],
                                       [("hT", i), ("hT", i)], scale=1.0 / ALPHA)

                    blocks = [(0, 512), (512, 512), (1024, 512), (1536, 512), (2048, 128)]
                    btiles = [list(range(t0 // 128, (t0 + n) // 128)) for (t0, n) in blocks]
                    a2_block(btiles[0])
                    for t in btiles[0]:
                        a2_s3(t)
                    wpp_bf = wbig[:, 0:2048].rearrange("p (k c) -> p k c", k=2)
                    bpg_bc = wbig[:, 2048:4096].bitcast(F32)
                    h2T = [wbig[:, 4096 + i * 1024:4096 + (i + 1) * 1024].rearrange("p (k c) -> p k c", k=8) for i in range(2)]
                    pT = [wbig[:, 6144 + i * 256:6144 + (i + 1) * 256].rearrange("p (k c) -> p k c", k=2) for i in range(2)]
                    pt = [wbig[:, 6656 + i * 512:6656 + (i + 1) * 512].bitcast(F32) for i in range(2)]
                    gs = [wbig[:, 8192 + i * 2048:8192 + (i + 1) * 2048].bitcast(F32) for i in range(2)]
                    ybuf = [wbig[:, 24576 + i * 2048:24576 + (i + 1) * 2048].bitcast(F32) for i in range(2)]
                    abi = [0]
                    sgi = [0]
                    ln2_pending = []
                    for G in range(NG):
                        slw = G % 2
                        nch = GRP[G][1]
                        if G == NG - 1:
                            for k in range(2):
                                load_cast(wpp_bf[:, k, :], wpp[k * 128:(k + 1) * 128, :], (1, 1024), "wpp", eng="act", wkeys_extra=SLOT0)
                            S.dma("bpg", lambda e: e.dma_start(out=bpg_bc[:], in_=bpg.partition_broadcast(128)), writes=["bpg_bc"] + SLOT0)
                            S.dma("p0", lambda e: e.dma_start(out=pt[0][:], in_=p_ap(0)), writes=[("pt", 0)] + SLOT0)
                        for bi, (t0, n) in enumerate(blocks):
                            ab = abi[0] % 2
                            abi[0] += 1
                            tiles = btiles[bi]
                            hkeys = [("hT", t) for t in tiles]
                            if G == 0 and bi + 1 < len(blocks):
                                a2_block(btiles[bi + 1])
                            for c in range(nch):
                                bg, bu = nb(), nb()

                                def fg(e, bg=bg, c=c, t0=t0, n=n, slw=slw):
                                    r = None
                                    for k in range(8):
                                        r = e.matmul(psb[bg][:, 0:n], lhsT=wgs[slw][:, k, c * 128:(c + 1) * 128],
                                                     rhs=hT[:, k, t0:t0 + n], start=(k == 0), stop=(k == 7))
                                    return r

                                def fu(e, bu=bu, c=c, t0=t0, n=n, slw=slw):
                                    r = None
                                    for k in range(8):
                                        r = e.matmul(psb[bu][:, 0:n], lhsT=wus[slw][:, k, c * 128:(c + 1) * 128],
                                                     rhs=hT[:, k, t0:t0 + n], start=(k == 0), stop=(k == 7))
                                    return r
                                S.op("pe", fg, reads=hkeys + [("wg", slw)], writes=[PK(bg)])
                                S.op("pe", fu, reads=hkeys + [("wu", slw)], writes=[PK(bu)])
                                sg = sgi[0] % 2
                                sgi[0] += 1
                                S.op("act", lambda e, bg=bg, sg=sg, n=n: e.activation(out=sgt[sg][:, 0:n], in_=psb[bg][:, 0:n], func=AF.Silu),
                                     reads=[PK(bg)], writes=[("sgt", sg)])
                                S.op("dve", lambda e, bu=bu, sg=sg, n=n, ab=ab, c=c: e.tensor_tensor(
                                    out=actb[ab][:, c, 0:n], in0=sgt[sg][:, 0:n], in1=psb[bu][:, 0:n], op=ALU.mult),
                                    reads=[PK(bu), ("sgt", sg)], writes=[("actb", ab, c)])
                            for ti, t in enumerate(tiles):
                                for half in range(2):
                                    bd = nb()

                                    def fd(e, bd=bd, ti=ti, half=half, ab=ab, slw=slw, nch=nch):
                                        r = None
                                        for c in range(nch):
                                            r = e.matmul(psb[bd][:], lhsT=actb[ab][:, c, ti * 128:(ti + 1) * 128],
                                                         rhs=wds[slw][:, c, half * 512:(half + 1) * 512],
                                                         start=(c == 0), stop=(c == nch - 1))
                                        return r
                                    S.op("pe", fd, reads=[("actb", ab, c) for c in range(nch)] + [("wd", slw)], writes=[PK(bd)])
                                    S.op("dve", lambda e, bd=bd, t=t, half=half: e.tensor_tensor(
                                        out=R[:, t, half * 512:(half + 1) * 512], in0=psb[bd][:],
                                        in1=R[:, t, half * 512:(half + 1) * 512], op=ALU.add),
                                        reads=[PK(bd), ("R", t)], writes=[("R", t)])
                                if G == NG - 1:
                                    ln_stats(R[:, t, :], [("R", t)], t % 2)
                                    if ln2_pending:
                                        tp_ = ln2_pending.pop()
                                        ln_apply(R[:, tp_, :], [("R", tp_)], R[:, tp_, :], ("R", tp_), tp_ % 2)
                                    ln2_pending.append(t)
                            if G == 0 and bi + 1 < len(blocks):
                                for t in btiles[bi + 1]:
                                    a2_s3(t)
                        if G == 0:
                            for k in range(8):
                                load_cast(w16[:, k, :], wpg[k * 128:(k + 1) * 128, :], (1, 1024), "w16", eng="pool")
                            S.dma("gbc", lambda e: e.dma_start(out=g_bc[:], in_=ln2g.partition_broadcast(128)), writes=["g_bc"])
                            S.dma("bbc", lambda e: e.dma_start(out=b_bc[:], in_=ln2b.partition_broadcast(128)), writes=["b_bc"])
                        if G + 2 < NG:
                            load_ffn_group(G + 2, ["pool"])
                    for tp_ in ln2_pending:
                        ln_apply(R[:, tp_, :], [("R", tp_)], R[:, tp_, :], ("R", tp_), tp_ % 2)


                    def c_s2(i):
                        par = i % 2
                        if i + 1 < NT:
                            S.dma("p%d" % ((i + 1) % 2), lambda e: e.dma_start(out=pt[(i + 1) % 2][:], in_=p_ap(i + 1)),
                                  writes=[("pt", (i + 1) % 2)] + SLOT0)
                        transposes_f32(R[:, i, :], [("R", i)], 8, lambda lo, hi: h2T[par][:, lo:hi, :],
                                       [("h2T", par, 0), ("h2T", par, 1)])
                        transposes_f32(pt[par][:], [("pt", par)], 2, lambda lo, hi: pT[par][:, lo:hi, :], [("pT", par)])

                    def c_s3(i):
                        par = i % 2
                        sl = par
                        for half in range(2):
                            bgt, bpp = nb(), nb()

                            def fgt(e, bgt=bgt, half=half):
                                r = None
                                for k in range(8):
                                    r = e.matmul(psb[bgt][:], lhsT=h2T[par][:, k, :], rhs=w16[:, k, half * 512:(half + 1) * 512],
                                                 start=(k == 0), stop=(k == 7))
                                return r

                            def fpp(e, bpp=bpp, half=half):
                                r = None
                                for k in range(2):
                                    r = e.matmul(psb[bpp][:], lhsT=pT[par][:, k, :], rhs=wpp_bf[:, k, half * 512:(half + 1) * 512],
                                                 start=(k == 0), stop=(k == 1))
                                return r
                            S.op("pe", fgt, reads=[("h2T", par, 0), ("h2T", par, 1), "w16"], writes=[PK(bgt)])
                            S.op("pe", fpp, reads=[("pT", par), "wpp"], writes=[PK(bpp)])
                            hs = slice(half * 512, (half + 1) * 512)
                            gk = ("gs", par, half)
                            S.op("dve", lambda e, bgt=bgt, hs=hs: e.tensor_tensor(out=gs[par][:, hs], in0=psb[bgt][:], in1=bpg_bc[:, hs], op=ALU.add),
                                 reads=[PK(bgt), "bpg_bc"], writes=[gk])
                            S.op("act", lambda e, hs=hs: e.activation(out=gs[par][:, hs], in_=gs[par][:, hs], func=AF.Sigmoid),
                                 reads=[gk], writes=[gk])
                            S.op("dve", lambda e, bpp=bpp, hs=hs: e.tensor_tensor(out=gs[par][:, hs], in0=gs[par][:, hs], in1=psb[bpp][:], op=ALU.mult),
                                 reads=[PK(bpp), gk], writes=[gk])
                            S.op("pool", lambda e, hs=hs: e.tensor_tensor(out=ybuf[sl][:, hs], in0=gs[par][:, hs], in1=R[:, i, hs], op=ALU.add),
                                 reads=[gk, ("R", i)], writes=[("ybuf", sl)])
                        S.dma("y%d" % sl, lambda e: e.dma_start(out=y_ap(i), in_=ybuf[sl][:]), reads=[("ybuf", sl)])

                    c_s2(0)
                    for i in range(NT):
                        if i + 1 < NT:
                            c_s2(i + 1)
                        c_s3(i)
                    S.wait_all_dma()
                    S.flush(block)
                S.after_barrier()

    print("sched: ops", S.nops, "waits", S.nwaits, flush=True)
    return nc


_CACHE = {}


def make_in_maps(inputs):
    tab, csn = host_tables()
    f = lambda a: np.ascontiguousarray(np.asarray(a, dtype=np.float32))
    maps = []
    for c in range(NCORES):
        m = {
            "xp": f(inputs["x_prompt"][c]), "xs": f(inputs["x_sample"][16 * c:16 * c + 16].reshape(128, D)),
            "pp": f(inputs["p_prompt"][0, c]), "ps": f(inputs["p_sample"][0, 16 * c:16 * c + 16].reshape(128, 256)),
            "sab": f(np.concatenate([inputs["state_hgrn"][0, 16 * c:16 * c + 16], inputs["state_ret"][0, 16 * c:16 * c + 16]],
                                    axis=1).transpose(0, 2, 1, 3).reshape(16, 128, 1024)),
            "lbl": f(inputs["lb_logits"]), "w_in": f(inputs["w_in"][0]), "w_out": f(inputs["w_out"][0]),
            "ang": f(inputs["a_norm_g"][0]), "bng": f(inputs["b_norm_g"][0]), "bnb": f(inputs["b_norm_b"][0]),
            "ln1g": f(inputs["ln1_g"][0]), "ln1b": f(inputs["ln1_b"][0]), "ln2g": f(inputs["ln2_g"][0]), "ln2b": f(inputs["ln2_b"][0]),
            "wg": f(inputs["w_ffn_gate"][0]), "wu": f(inputs["w_ffn_up"][0]), "wd": f(inputs["w_ffn_down"][0]),
            "wpp": f(inputs["w_ple_proj"][0]), "wpg": f(inputs["w_ple_gate"][0]), "bpg": f(inputs["b_ple_gate"][0]),
            "tab": tab, "csn": csn,
        }
        maps.append(m)
    return maps


def kernel(**inputs):
    if "nc" not in _CACHE:
        _CACHE["nc"] = build(debug=False)
    nc = _CACHE["nc"]
    maps = make_in_maps(inputs)
    res = run_bass_kernel_spmd(nc, maps, core_ids=list(range(NCORES)))
    rs = res.results
    y_prompt = np.stack([rs[c]["yp"] for c in range(NCORES)], 0).astype(np.float32)
    y_sample = np.concatenate([rs[c]["ys"].reshape(16, 8, D) for c in range(NCORES)], 0).astype(np.float32)
    nsa_p = np.stack([rs[c]["nsa_p"] for c in range(NCORES)], 0)[None].astype(np.float32)
    nsb_p = np.stack([rs[c]["nsb_p"] for c in range(NCORES)], 0)[None].astype(np.float32)
    nsab = np.concatenate([rs[c]["nsab_s"] for c in range(NCORES)], 0).reshape(128, 128, 8, 128).transpose(0, 2, 1, 3)
    nsa_s = np.ascontiguousarray(nsab[:, 0:4])[None].astype(np.float32)
    nsb_s = np.ascontiguousarray(nsab[:, 4:8])[None].astype(np.float32)
    return (y_prompt, y_sample, nsa_p, nsb_p, nsa_s, nsb_s)
```

```python
import contextlib
import numpy as np
import concourse.bass as bass
import concourse.mybir as mybir
from concourse.bass_utils import run_bass_kernel_spmd

F32 = mybir.dt.float32
BF16 = mybir.dt.bfloat16
AF = mybir.ActivationFunctionType
ALU = mybir.AluOpType

NCORES = 8
D = 1024
DFF = 2816
NT = 17
NTOK = NT * 128
ALPHA = 2.0 ** 0.25
EPS = 1e-5
GAM = [1.0 - 2.0 ** (-5.0 - h) for h in range(4)]

O_ID = 0
O_TRI = {"P": 128, "S": 384}
O_TRIU = {"P": 256, "S": 512}
O_IND = {"P": 640, "S": 656}
O_MB = {"P": 672, "S": 1184}
O_GQ = {"P": 1696, "S": 2208}
O_KS = {"P": 2720, "S": 2724}
TABW = 2728


def host_tables():
    tab = np.zeros((128, TABW), np.float64)
    tab[:, O_ID:O_ID + 128] = np.eye(128)
    s = np.arange(128)
    for kind, C in (("P", 128), ("S", 8)):
        loc = s % C
        blk = s // C
        same = blk[:, None] == blk[None, :]
        tri = same & (s[:, None] <= s[None, :])
        triu = same & (s[:, None] > s[None, :])
        tab[:, O_TRI[kind]:O_TRI[kind] + 128] = tri
        tab[:, O_TRIU[kind]:O_TRIU[kind] + 128] = triu
        ind = np.zeros((128, 16))
        ind[s, blk] = 1.0
        tab[:, O_IND[kind]:O_IND[kind] + 16] = ind
        for h in range(4):
            g = GAM[h]
            mb = tri * (g ** (-(loc[:, None] + 1.0))) * (128.0 ** -0.5)
            tab[:, O_MB[kind] + h * 128:O_MB[kind] + (h + 1) * 128] = mb
            tab[:, O_GQ[kind] + h * 128:O_GQ[kind] + (h + 1) * 128] = (g ** (loc + 1.0))[None, :]
            tab[:, O_KS[kind] + h] = (g ** (C - 1.0 - loc)) * (128.0 ** -0.5)
    csn = np.zeros((NT, 128, 256), np.float64)
    inv = 10000.0 ** (-np.arange(64, dtype=np.float64) / 64.0)
    for i in range(NT):
        if i < 16:
            pos = i * 128 + np.arange(128, dtype=np.float64)
        else:
            pos = 16384.0 + (np.arange(128) % 8).astype(np.float64)
        ang = pos[:, None] * inv[None, :]
        c, sn = np.cos(ang), np.sin(ang)
        csn[i, :, 0:64] = c
        csn[i, :, 64:128] = c
        csn[i, :, 128:192] = -sn
        csn[i, :, 192:256] = sn
    return tab.astype(np.float32), csn.astype(np.float32)


class Sched:
    ENG = ("pe", "act", "dve", "pool", "sp")

    def __init__(self, nc, sems, free_chans):
        self.nc = nc
        self.sems = sems
        self.free_chans = list(free_chans)
        self.q = {e: [] for e in self.ENG}
        self.cnt = {e: 0 for e in self.ENG}
        self.waited = {e: {} for e in self.ENG}
        self.lastw = {}
        self.readers = {}
        self.nwaits = 0
        self.nops = {e: 0 for e in self.ENG}
        self.know = {}

    def chan(self, name):
        if name not in self.sems:
            self.sems[name] = self.free_chans.pop()
            self.cnt[name] = 0
        return name

    def _deps(self, eng, reads, writes):
        d = {}

        def add(tok):
            if tok is None:
                return
            sk, v = tok
            if d.get(sk, 0) < v:
                d[sk] = v
        for r in reads:
            add(self.lastw.get(r))
        for w in writes:
            add(self.lastw.get(w))
            for t in self.readers.get(w, ()):
                add(t)
        kn = self.waited[eng]
        waits = []
        for sk, v in sorted(d.items(), key=lambda kv: -len(self.know.get((kv[0], kv[1]), ()))):
            if kn.get(sk, 0) >= v:
                continue
            waits.append((sk, v))
            kn[sk] = v
            for k2, v2 in self.know.get((sk, v), {}).items():
                if kn.get(k2, 0) < v2:
                    kn[k2] = v2
        self.nwaits += len(waits)
        return waits

    def _finish(self, tok, reads, writes):
        for r in reads:
            self.readers.setdefault(r, []).append(tok)
        for w in writes:
            self.lastw[w] = tok
            self.readers[w] = []

    @staticmethod
    def _excl(reads, writes):
        ps = [r for r in reads if isinstance(r, tuple) and r[0] == "ps"]
        if ps:
            reads = [r for r in reads if not (isinstance(r, tuple) and r[0] == "ps")]
            writes = list(writes) + ps
        return reads, writes

    def op(self, eng, fn, reads=(), writes=()):
        reads, writes = self._excl(reads, writes)
        waits = self._deps(eng, reads, writes)
        self.cnt[eng] += 1
        self.nops[eng] += 1
        tok = (eng, self.cnt[eng])
        self.know[tok] = dict(self.waited[eng])
        sem = self.sems[eng]
        sems = self.sems

        def emit(e):
            for sk, v in waits:
                e.wait_ge(sems[sk], v)
            fn(e).then_inc(sem, 1)
        self.q[eng].append(emit)
        self._finish(tok, reads, writes)

    def dma(self, chan, fn, reads=(), writes=(), eng="sp"):
        self.chan(chan)
        waits = self._deps(eng, reads, writes)
        self.cnt[chan] += 16
        self.nops[eng] += 1
        tok = (chan, self.cnt[chan])
        self.know[tok] = dict(self.waited[eng])
        sem = self.sems[chan]
        sems = self.sems

        def emit(e):
            for sk, v in waits:
                e.wait_ge(sems[sk], v)
            fn(e).then_inc(sem, 16)
        self.q[eng].append(emit)
        self._finish(tok, reads, writes)

    def wait_all_dma(self, eng="sp"):
        waits = [(sk, v) for sk, v in self.cnt.items() if sk not in self.ENG and v > 0
                 and self.waited[eng].get(sk, 0) < v]
        for sk, v in waits:
            self.waited[eng][sk] = v
        sems = self.sems

        def emit(e):
            for sk, v in waits:
                e.wait_ge(sems[sk], v)
        self.q[eng].append(emit)

    def flush(self, block):
        qs = self.q

        @block.tensor
        def _(e):
            for f in qs["pe"]:
                f(e)

        @block.scalar
        def _(e):
            for f in qs["act"]:
                f(e)

        @block.vector
        def _(e):
            for f in qs["dve"]:
                f(e)

        @block.gpsimd
        def _(e):
            for f in qs["pool"]:
                f(e)

        @block.sync
        def _(e):
            for f in qs["sp"]:
                f(e)
        self.q = {e: [] for e in self.ENG}

    def after_barrier(self):
        for e in self.ENG:
            for sk, v in self.cnt.items():
                self.waited[e][sk] = v


def run_interleaved(gens):
    gens = [g for g in gens if g is not None]
    while gens:
        for g in list(gens):
            try:
                next(g)
            except StopIteration:
                gens.remove(g)


def run_pair(b, f):
    b_alive, f_alive, hold = True, f is not None, False
    while b_alive or f_alive:
        if b_alive and not (hold and f_alive):
            try:
                if next(b) == "TAIL":
                    hold = True
            except StopIteration:
                b_alive = False
        if f_alive:
            try:
                next(f)
            except StopIteration:
                f_alive = False


def build(debug=False):
    nc = bass.Bass("TRN2", target_bir_lowering=False, dynamic_dma_scratch_size=512)

    def din(name, shape, dt=F32):
        return nc.dram_tensor(name, list(shape), dt, kind="ExternalInput").ap()

    def dout(name, shape):
        return nc.dram_tensor(name, list(shape), F32, kind="ExternalOutput").ap()

    xp = din("xp", [2048, D]); xs = din("xs", [128, D])
    pp_d = din("pp", [2048, 256]); ps_d = din("ps", [128, 256])
    sab = din("sab", [16, 128, 1024])
    lbl = din("lbl", [2, 512])
    w_in = din("w_in", [D, 4096]); w_out = din("w_out", [D, D])
    ang = din("ang", [512]); bng = din("bng", [512]); bnb = din("bnb", [512])
    ln1g = din("ln1g", [D]); ln1b = din("ln1b", [D]); ln2g = din("ln2g", [D]); ln2b = din("ln2b", [D])
    wg = din("wg", [D, DFF]); wu = din("wu", [D, DFF]); wd = din("wd", [DFF, D])
    wpp = din("wpp", [256, D]); wpg = din("wpg", [D, D]); bpg = din("bpg", [D])
    tab_d = din("tab", [128, TABW]); csn_d = din("csn", [NT, 128, 256])

    yp = dout("yp", [2048, D]); ys = dout("ys", [128, D])
    nsa_p = dout("nsa_p", [4, 128, 128]); nsb_p = dout("nsb_p", [4, 128, 128])
    nsab_s = dout("nsab_s", [16, 128, 1024])
    dbg = {}
    if debug:
        for nm, shp in (("d_h", [NT, 128, D]), ("d_r2", [NT, 128, D])):
            dbg[nm] = dout(nm, shp)

    def x_ap(i):
        return xp[i * 128:(i + 1) * 128, :] if i < 16 else xs

    def y_ap(i):
        return yp[i * 128:(i + 1) * 128, :] if i < 16 else ys

    def p_ap(i):
        return pp_d[i * 128:(i + 1) * 128, :] if i < 16 else ps_d

    top = contextlib.ExitStack()
    with top:
        sems = {}
        for n in Sched.ENG:
            sems[n] = top.enter_context(nc.semaphore(n))
        free_chans = [top.enter_context(nc.semaphore("ch%d" % i)) for i in range(60)]
        S = Sched(nc, sems, free_chans)

        def sbt(stack, name, shape, dt=F32):
            return stack.enter_context(nc.sbuf_tensor(name, list(shape), dt))

        psb = [top.enter_context(nc.psum_tensor("psb%d" % i, [128, 512], F32)) for i in range(8)]
        pinned = set()
        rr = [0]

        def nb():
            while True:
                i = rr[0] % 8
                rr[0] += 1
                if i not in pinned:
                    return i

        def PK(i):
            return ("ps", i)

        def v4(ap):
            return ap.rearrange("p (h d) -> p h d", h=4)

        hT = sbt(top, "hT", [128, 8, NTOK], BF16)
        stage = [sbt(top, "stage%d" % i, [128, 1024]) for i in range(2)]
        tab = sbt(top, "tabs", [128, TABW])
        identb = sbt(top, "identb", [128, 128], BF16)
        xsl = [sbt(top, "xsl%d" % i, [128, D]) for i in range(2)]
        cm05 = sbt(top, "cm05", [128, 8])
        one_c = sbt(top, "one_c", [128, 1])
        w16 = sbt(top, "w16", [128, 8, D], BF16)
        wbig = sbt(top, "wbig", [128, 32768], BF16)
        w_in_bf = wbig[:].rearrange("p (k c) -> p k c", k=8)
        wgs = [wbig[:, s_ * 12288:s_ * 12288 + 4096].rearrange("p (k c) -> p k c", k=8) for s_ in range(2)]
        wus = [wbig[:, s_ * 12288 + 4096:s_ * 12288 + 8192].rearrange("p (k c) -> p k c", k=8) for s_ in range(2)]
        wds = [wbig[:, s_ * 12288 + 8192:s_ * 12288 + 12288].rearrange("p (k c) -> p k c", k=4) for s_ in range(2)]
        r_t = [wbig[:, 24576 + i * 2048:24576 + (i + 1) * 2048].bitcast(F32) for i in range(2)]
        actb = [wbig[:, 28672 + i * 2048:28672 + (i + 1) * 2048].rearrange("p (k c) -> p k c", k=4) for i in range(2)]
        WIN_KEYS = [("win", cb) for cb in range(4)]
        SLOT0 = [("wg", 0), ("wu", 0), ("wd", 0)]
        identf = tab[:, O_ID:O_ID + 128]
        stg_i = [0]

        hTf = hT[:].rearrange("p a b -> p (a b)")
        NXS = 6
        xstage = [hTf[:, i * 2048:(i + 1) * 2048].bitcast(F32) for i in range(NXS)]

        def load_cast(dst_ap, src_ap, shape3, wkey, eng="pool", wide=False, wkeys_extra=()):
            nsl = 2 + NXS if wide else 2
            sl = stg_i[0] % nsl
            stg_i[0] += 1
            a, b = shape3
            if sl < 2:
                st_ap = stage[sl][:, 0:a * b]
                keys = [("stage", sl)]
            else:
                st_ap = xstage[sl - 2][:, 0:a * b]
                keys = [("stage", sl), "hTalias"]
            st_v = st_ap.rearrange("p (a b) -> p a b", a=a) if a > 1 else st_ap
            S.dma("stg%d" % sl, lambda e: e.dma_start(out=st_v, in_=src_ap), writes=keys)
            if eng == "act":
                S.op("act", lambda e: e.activation(out=dst_ap, in_=st_v, func=AF.Copy), reads=[("stage", sl)],
                     writes=[wkey] + list(wkeys_extra))
            else:
                S.op(eng, lambda e: e.tensor_copy(out=dst_ap, in_=st_v), reads=[("stage", sl)], writes=[wkey] + list(wkeys_extra))

        GRP = [(0, 4), (512, 4), (2560, 2), (1024, 4), (1536, 4), (2048, 4)]
        NG = 6
        cast_rr = [0]

        def gen_load_ffn_group(G, engs, extra=()):
            slw = G % 2
            c0, nch = GRP[G]
            w = nch * 128
            ex = list(extra)

            def eng():
                cast_rr[0] += 1
                return engs[cast_rr[0] % len(engs)]
            for kk in range(0, 8, 2):
                load_cast(wgs[slw][:, kk:kk + 2, 0:w],
                          wg[kk * 128:(kk + 2) * 128, c0:c0 + w].rearrange("(k p) c -> p k c", p=128),
                          (2, w), ("wg", slw), eng(), wkeys_extra=ex)
                yield
            for kk in range(0, 8, 2):
                load_cast(wus[slw][:, kk:kk + 2, 0:w],
                          wu[kk * 128:(kk + 2) * 128, c0:c0 + w].rearrange("(k p) c -> p k c", p=128),
                          (2, w), ("wu", slw), eng(), wkeys_extra=ex)
                yield
            for c in range(nch):
                load_cast(wds[slw][:, c, :], wd[c0 + c * 128:c0 + (c + 1) * 128, :], (1, 1024), ("wd", slw), eng(), wkeys_extra=ex)
                yield

        def load_ffn_group(G, engs, extra=()):
            for _ in gen_load_ffn_group(G, engs, extra):
                pass

        a1 = contextlib.ExitStack()
        with a1:
            oml = sbt(a1, "oml", [128, 512])
            l0 = sbt(a1, "l0", [128, 512]); l1 = sbt(a1, "l1", [128, 512])
            bnb_bc = sbt(a1, "bnb_bc", [128, 512])
            csn = [sbt(a1, "csn%d" % i, [128, 256]) for i in range(2)]
            xT = sbt(a1, "xT", [128, 8, 128], BF16)
            Tbig = sbt(a1, "Tbig", [128, 13 * 512])
            T = [Tbig[:, i * 512:(i + 1) * 512] for i in range(13)]
            kte_all = [sbt(a1, "kte_all%d" % i, [128, 1024], BF16) for i in range(2)]
            v_all = [sbt(a1, "v_all%d" % i, [128, 1024], BF16) for i in range(2)]
            qd = [sbt(a1, "qd%d" % i, [128, 512], BF16) for i in range(2)]
            kd = [sbt(a1, "kd%d" % i, [128, 512], BF16) for i in range(2)]
            qr = [sbt(a1, "qr%d" % i, [128, 512], BF16) for i in range(2)]
            kr = [sbt(a1, "kr%d" % i, [128, 512], BF16) for i in range(2)]
            GA = [sbt(a1, "GA%d" % i, [128, 512]) for i in range(2)]
            GB = [sbt(a1, "GB%d" % i, [128, 512]) for i in range(2)]
            HB = [sbt(a1, "HB%d" % i, [128, 512]) for i in range(2)]
            decA = [sbt(a1, "decA%d" % i, [128, 64]) for i in range(2)]
            qkT = {"A": sbt(a1, "qkT_A", [128, 8, 128], BF16), "B": sbt(a1, "qkT_B", [128, 8, 128], BF16)}
            scm = {"A": sbt(a1, "scm_A", [128, 4, 128], BF16), "B": sbt(a1, "scm_B", [128, 4, 128], BF16)}
            mixin = sbt(a1, "mixin", [128, D], BF16)
            Sm = sbt(a1, "Sm", [128, 8, 128]); Sbf = sbt(a1, "Sbf", [128, 8, 128], BF16)
            st6 = sbt(a1, "st6", [128, 8, 6]); mv = sbt(a1, "mv", [128, 8, 2])
            sm1 = sbt(a1, "sm1", [128, 8]); rstd = sbt(a1, "rstd", [128, 8])
            S0 = [Tbig[:, (2 + 2 * i) * 512:(4 + 2 * i) * 512].rearrange("p (h v) -> p h v", h=8) for i in range(3)]
            S0K = [["T%d" % (2 + 2 * i), "T%d" % (3 + 2 * i)] for i in range(3)]
            S0bf = [Tbig[:, (8 + i) * 512:(9 + i) * 512].bitcast(BF16).rearrange("p (h v) -> p h v", h=8) for i in range(2)]
            S0bfK = [["T8"], ["T9"]]
            ktem = [Tbig[:, (10 + i) * 512:(11 + i) * 512].bitcast(BF16) for i in range(2)]
            ktemK = [["T10"], ["T11"]]
            zerob = sbt(a1, "zerob", [128, 128], BF16)

            with nc.Block(no_gpsimd_drain=True) as block:
                S.dma("tab", lambda e: e.dma_start(out=tab[:], in_=tab_d), writes=["tab"])
                S.dma("l0", lambda e: e.dma_start(out=l0[:], in_=lbl[0, :].partition_broadcast(128)), writes=["l0"])
                S.dma("l1", lambda e: e.dma_start(out=l1[:], in_=lbl[1, :].partition_broadcast(128)), writes=["l1"])
                S.dma("bnb", lambda e: e.dma_start(out=bnb_bc[:], in_=bnb.partition_broadcast(128)), writes=["bnb_bc"])
                S.op("pool", lambda e: e.tensor_copy(out=identb[:], in_=identf), reads=["tab"], writes=["identb"])
                S.op("pool", lambda e: e.memset(cm05[:], -0.5), writes=["cm05"])
                S.op("pool", lambda e: e.memset(one_c[:], 1.0), writes=["one_c"])
                S.op("pool", lambda e: e.memset(zerob[:], 0.0), writes=["zerob"])
                S.op("pool", lambda e: e.memset(Sm[:], 0.0), writes=["SmA", "SmB"])
                S.op("pool", lambda e: e.memset(Sbf[:], 0.0), writes=["SbfA", "SbfB"])
                S.op("pool", lambda e: e.tensor_tensor(out=l0[:], in0=l0[:], in1=l1[:], op=ALU.subtract),
                     reads=["l0", "l1"], writes=["l0"])
                S.op("act", lambda e: e.activation(out=oml[:], in_=l0[:], func=AF.Sigmoid, scale=-1.0),
                     reads=["l0"], writes=["oml"])
                ang_bc, bng_bc = l0, l1
                S.dma("l0", lambda e: e.dma_start(out=ang_bc[:], in_=ang.partition_broadcast(128)), writes=["l0"])
                S.dma("l1", lambda e: e.dma_start(out=bng_bc[:], in_=bng.partition_broadcast(128)), writes=["l1"])

                def load_tile_inputs(i):
                    sl = i % 2
                    S.dma("x%d" % sl, lambda e: e.dma_start(out=xsl[sl][:], in_=x_ap(i)), writes=[("xsl", sl)])
                    S.dma("csn%d" % sl, lambda e: e.dma_start(out=csn[sl][:], in_=csn_d[i]), writes=[("csn", sl)])

                load_tile_inputs(0)

                win_emitted = {}

                def win_loader():
                    ci = 0
                    for cb in range(4):
                        for k in range(8):
                            load_cast(w_in_bf[:, k, cb * 1024:(cb + 1) * 1024],
                                      w_in[k * 128:(k + 1) * 128, cb * 1024:(cb + 1) * 1024], (1, 1024), ("win", cb),
                                      eng=("act" if ci % 2 == 0 else "dve"), wide=True)
                            ci += 1
                            win_emitted[cb] = win_emitted.get(cb, 0) + 1
                            if cb > 0 and k % 4 == 3:
                                yield
                    stg_i[0] = 0

                def tabs_for(kind):
                    return dict(
                        TRI=tab[:, O_TRI[kind]:O_TRI[kind] + 128], TRIU=tab[:, O_TRIU[kind]:O_TRIU[kind] + 128],
                        IND=tab[:, O_IND[kind]:O_IND[kind] + 16], MB=tab[:, O_MB[kind]:O_MB[kind] + 512],
                        GQ=tab[:, O_GQ[kind]:O_GQ[kind] + 512], KS=tab[:, O_KS[kind]:O_KS[kind] + 4])

                def front(i):
                    kind = "S" if i == 16 else "P"
                    tb_ = tabs_for(kind)
                    TRI, TRIU, IND, KS = tb_["TRI"], tb_["TRIU"], tb_["IND"], tb_["KS"]
                    sl = i % 2
                    par = i % 2
                    P = str(par)
                    xt = xsl[sl]
                    if i + 1 < NT:
                        load_tile_inputs(i + 1)
                    for half in range(2):
                        bk = nb()

                        def f(e, bk=bk, half=half):
                            r = None
                            for q in range(4):
                                kq = half * 4 + q
                                r = e.transpose(psb[bk][:, q * 128:(q + 1) * 128], xt[:, kq * 128:(kq + 1) * 128], identf)
                            return r
                        S.op("pe", f, reads=[("xsl", sl), "tab"], writes=[PK(bk)])
                        dst = xT[:, half * 4:(half + 1) * 4, :].rearrange("p a b -> p (a b)")
                        if half == 0:
                            S.op("act", lambda e, bk=bk, dst=dst: e.activation(out=dst, in_=psb[bk][:], func=AF.Copy),
                                 reads=[PK(bk)], writes=[("xT", 0)])
                        else:
                            S.op("act", lambda e, bk=bk, dst=dst: e.activation(out=dst, in_=psb[bk][:], func=AF.Copy),
                                 reads=[PK(bk)], writes=[("xT", 1)])
                        yield

                    def proj(b):
                        assert win_emitted.get(b // 2, 0) == 8, "w_in block not loaded before use"
                        bk = nb()

                        def f(e):
                            r = None
                            for k in range(8):
                                r = e.matmul(psb[bk][:], lhsT=xT[:, k, :], rhs=w_in_bf[:, k, b * 512:(b + 1) * 512],
                                             start=(k == 0), stop=(k == 7))
                            return r
                        S.op("pe", f, reads=[("xT", 0), ("xT", 1), ("win", b // 2)], writes=[PK(bk)])
                        return bk

                    cs = csn[sl][:, 0:128]
                    sn = csn[sl][:, 128:256]

                    def rope(b, t_a, t_b, t_c, outbf, outkey):
                        bk = proj(b)
                        ka, kb_, kc = "T%d" % t_a, "T%d" % t_b, "T%d" % t_c
                        S.op("dve", lambda e: e.tensor_tensor(out=v4(T[t_a][:]), in0=v4(psb[bk][:]),
                                                              in1=cs[:, None, :].to_broadcast([128, 4, 128]), op=ALU.mult),
                             reads=[PK(bk), ("csn", sl)], writes=[ka])
                        S.op("act", lambda e: e.activation(out=T[t_b][:], in_=psb[bk][:], func=AF.Copy),
                             reads=[PK(bk)], writes=[kb_])
                        S.op("dve", lambda e: e.tensor_tensor(out=v4(T[t_c][:])[:, :, 0:64], in0=v4(T[t_b][:])[:, :, 64:128],
                                                              in1=sn[:, None, 0:64].to_broadcast([128, 4, 64]), op=ALU.mult),
                             reads=[kb_, ("csn", sl)], writes=[kc])
                        S.op("dve", lambda e: e.tensor_tensor(out=v4(T[t_c][:])[:, :, 64:128], in0=v4(T[t_b][:])[:, :, 0:64],
                                                              in1=sn[:, None, 64:128].to_broadcast([128, 4, 64]), op=ALU.mult),
                             reads=[kb_, ("csn", sl)], writes=[kc])
                        S.op("pool", lambda e: e.tensor_tensor(out=outbf[:], in0=T[t_a][:], in1=T[t_c][:], op=ALU.add),
                             reads=[ka, kc], writes=[outkey])

                    bk = proj(1)
                    S.op("act", lambda e, bk=bk: e.activation(out=T[1][:], in_=psb[bk][:], func=AF.Sigmoid, scale=-1.0),
                         reads=[PK(bk)], writes=["T1"])
                    S.op("dve", lambda e: e.tensor_tensor(out=T[4][:], in0=T[1][:], in1=oml[:], op=ALU.mult),
                         reads=["T1", "oml"], writes=["T4"])
                    S.op("act", lambda e: e.activation(out=T[5][:], in_=T[4][:], func=AF.Ln, scale=-1.0, bias=one_c[:]),
                         reads=["T4", "one_c"], writes=["T5"])
                    yield
                    rope(5, 10, 11, 12, kr[par], "kr" + P)
                    for h in range(4):
                        S.op("dve", lambda e, h=h: e.tensor_scalar(
                            out=kte_all[par][:, 512 + h * 128:512 + (h + 1) * 128], in0=kr[par][:, h * 128:(h + 1) * 128],
                            scalar1=KS[:, h:h + 1], scalar2=None, op0=ALU.mult),
                            reads=["kr" + P, "tab"], writes=["kteB" + P])
                    yield
                    bk6 = proj(6)
                    S.op("act", lambda e: e.activation(out=v_all[par][:, 512:1024], in_=psb[bk6][:], func=AF.Copy),
                         reads=[PK(bk6)], writes=["vB" + P])
                    yield
                    bk0 = proj(0)
                    S.op("act", lambda e: e.activation(out=T[0][:], in_=psb[bk0][:], func=AF.Silu),
                         reads=[PK(bk0)], writes=["T0"])
                    yield
                    bkb, bks, bkd = nb(), nb(), nb()
                    S.op("pe", lambda e: e.matmul(psb[bkb][:], lhsT=TRI, rhs=T[5][:], start=True, stop=True),
                         reads=["T5", "tab"], writes=[PK(bkb)])
                    S.op("act", lambda e: e.activation(out=T[6][:], in_=psb[bkb][:], func=AF.Exp),
                         reads=[PK(bkb)], writes=["T6"])
                    S.op("act", lambda e: e.activation(out=T[7][:], in_=psb[bkb][:], func=AF.Exp, scale=-1.0),
                         reads=[PK(bkb)], writes=["T7"])
                    yield
                    S.op("pe", lambda e: e.matmul(psb[bks][:], lhsT=TRIU, rhs=T[5][:], start=True, stop=True),
                         reads=["T5", "tab"], writes=[PK(bks)])
                    S.op("act", lambda e: e.activation(out=T[8][:], in_=psb[bks][:], func=AF.Exp),
                         reads=[PK(bks)], writes=["T8"])

                    def fdec(e):
                        r = None
                        for h in range(4):
                            r = e.matmul(psb[bkd][:, h * 16:(h + 1) * 16], lhsT=T[5][:, h * 128:(h + 1) * 128], rhs=IND,
                                         start=True, stop=True)
                        return r
                    S.op("pe", fdec, reads=["T5", "tab"], writes=[PK(bkd)])
                    S.op("act", lambda e: e.activation(out=decA[par][:], in_=psb[bkd][:, 0:64], func=AF.Exp),
                         reads=[PK(bkd)], writes=["decA" + P])
                    S.op("dve", lambda e: e.tensor_tensor(out=qd[par][:], in0=T[0][:], in1=T[6][:], op=ALU.mult),
                         reads=["T0", "T6"], writes=["qd" + P])
                    S.op("dve", lambda e: e.tensor_tensor(out=kd[par][:], in0=T[4][:], in1=T[7][:], op=ALU.mult),
                         reads=["T4", "T7"], writes=["kd" + P])
                    S.op("dve", lambda e: e.tensor_tensor(out=kte_all[par][:, 0:512], in0=T[4][:], in1=T[8][:], op=ALU.mult),
                         reads=["T4", "T8"], writes=["kteA" + P])
                    yield
                    bk2 = proj(2)
                    S.op("act", lambda e: e.activation(out=v_all[par][:, 0:512], in_=psb[bk2][:], func=AF.Copy),
                         reads=[PK(bk2)], writes=["vA" + P])
                    yield
                    rope(4, 2, 3, 9, qr[par], "qr" + P)
                    yield
                    bk3 = proj(3)
                    S.op("act", lambda e: e.activation(out=GA[par][:], in_=psb[bk3][:], func=AF.Silu),
                         reads=[PK(bk3)], writes=["GA" + P])
                    S.op("pool", lambda e: e.tensor_tensor(out=GA[par][:], in0=GA[par][:], in1=ang_bc[:], op=ALU.mult),
                         reads=["GA" + P, "l0"], writes=["GA" + P])
                    yield

                    bk7 = proj(7)
                    S.op("act", lambda e: e.activation(out=GB[par][:], in_=psb[bk7][:], func=AF.Silu),
                         reads=[PK(bk7)], writes=["GB" + P])
                    S.op("pool", lambda e: e.tensor_tensor(out=HB[par][:], in0=GB[par][:], in1=bnb_bc[:], op=ALU.mult),
                         reads=["GB" + P, "bnb_bc"], writes=["HB" + P])
                    S.op("pool", lambda e: e.tensor_tensor(out=GB[par][:], in0=GB[par][:], in1=bng_bc[:], op=ALU.mult),
                         reads=["GB" + P, "l1"], writes=["GB" + P])
                    yield

                def back(i):
                    kind = "S" if i == 16 else "P"
                    tb_ = tabs_for(kind)
                    TRI, IND, MB, GQ = tb_["TRI"], tb_["IND"], tb_["MB"], tb_["GQ"]
                    par = i % 2
                    P = str(par)
                    CB = 8 if kind == "S" else 128
                    ktea, va = kte_all[par], v_all[par]
                    pins = []
                    NS = 3
                    if kind == "S":

                        def load_state(j):
                            s3 = j % NS
                            S.dma("s0a%d" % s3, lambda e: e.dma_start(
                                out=S0[s3].rearrange("p a b -> p (a b)"), in_=sab[j]), writes=S0K[s3])
                        for j in range(NS):
                            load_state(j)

                    def group(g):
                        go = 0 if g == "A" else 4
                        fo = 0 if g == "A" else 512
                        qsrc, ksrc = (qd[par], kd[par]) if g == "A" else (qr[par], kr[par])
                        qkey, kkey = ("qd" + P, "kd" + P) if g == "A" else ("qr" + P, "kr" + P)
                        qk = qkT[g]
                        if kind == "P":
                            bkv = nb()

                            def fkv(e):
                                r = None
                                for h in range(4):
                                    r = e.matmul(psb[bkv][:, h * 128:(h + 1) * 128],
                                                 lhsT=ktea[:, fo + h * 128:fo + (h + 1) * 128],
                                                 rhs=va[:, fo + h * 128:fo + (h + 1) * 128], start=True, stop=True)
                                return r
                            S.op("pe", fkv, reads=["kte" + g + P, "v" + g + P], writes=[PK(bkv)])
                            for h in range(4):
                                sc = decA[par][:, h * 16:h * 16 + 1] if g == "A" else float(GAM[h] ** CB)
                                S.op("dve", lambda e, h=h, sc=sc: e.scalar_tensor_tensor(
                                    out=Sm[:, go + h, :], in0=Sm[:, go + h, :], scalar=sc,
                                    in1=psb[bkv][:, h * 128:(h + 1) * 128], op0=ALU.mult, op1=ALU.add),
                                    reads=[PK(bkv), "Sm" + g, "decA" + P], writes=["Sm" + g])
                            yield
                        bk = nb()
                        pb = psb[bk][:].bitcast(BF16)

                        def ftr(e):
                            r = None
                            for h in range(4):
                                r = e.transpose(pb[:, h * 128:(h + 1) * 128], qsrc[:, h * 128:(h + 1) * 128], identb[:])
                            for h in range(4):
                                r = e.transpose(pb[:, (4 + h) * 128:(5 + h) * 128], ksrc[:, h * 128:(h + 1) * 128], identb[:])
                            return r
                        S.op("pe", ftr, reads=[qkey, kkey, "identb"], writes=[PK(bk)])
                        qkf = qk[:].rearrange("p a b -> p (a b)")
                        if g == "A":
                            S.op("act", lambda e: e.activation(out=qkf, in_=pb, func=AF.Copy), reads=[PK(bk)], writes=["qkT" + g])
                        else:
                            S.op("dve", lambda e: e.tensor_tensor(out=qkf[:, 0:512], in0=pb[:, 0:512], in1=GQ, op=ALU.mult),
                                 reads=[PK(bk), "tab"], writes=["qkT" + g])
                            S.op("act", lambda e: e.activation(out=qkf[:, 512:1024], in_=pb[:, 512:1024], func=AF.Copy),
                                 reads=[PK(bk)], writes=["qkT" + g + "k"])
                        qk_reads = ["qkT" + g] + (["qkT" + g + "k"] if g == "B" else [])
                        yield
                        bsc = nb()

                        def fsc(e):
                            r = None
                            for h in range(4):
                                r = e.matmul(psb[bsc][:, h * 128:(h + 1) * 128], lhsT=qk[:, 4 + h, :], rhs=qk[:, h, :],
                                             start=True, stop=True)
                            return r
                        S.op("pe", fsc, reads=qk_reads, writes=[PK(bsc)])
                        scf = scm[g][:].rearrange("p a b -> p (a b)")
                        if g == "A":
                            S.op("dve", lambda e: e.tensor_tensor(out=scm[g][:], in0=v4(psb[bsc][:]),
                                                                  in1=TRI[:, None, :].to_broadcast([128, 4, 128]), op=ALU.mult),
                                 reads=[PK(bsc), "tab"], writes=["scm" + g])
                        else:
                            S.op("dve", lambda e: e.tensor_tensor(out=scf, in0=psb[bsc][:], in1=MB, op=ALU.mult),
                                 reads=[PK(bsc), "tab"], writes=["scm" + g])
                        yield
                        bo = nb()
                        pinned.add(bo)
                        pins.append(bo)
                        bos[g] = bo

                        def fo_(e):
                            r = None
                            if kind == "S":
                                e.matmul(psb[bo][:], lhsT=zerob[:], rhs=va[:, fo:fo + 512], start=True, stop=False,
                                         skip_group_check=True)
                            for h in range(4):
                                r = e.matmul(psb[bo][:, h * 128:(h + 1) * 128], lhsT=scm[g][:, h, :],
                                             rhs=va[:, fo + h * 128:fo + (h + 1) * 128], start=(kind == "P"), stop=False,
                                             skip_group_check=(kind == "S"))
                                if kind == "P":
                                    r = e.matmul(psb[bo][:, h * 128:(h + 1) * 128], lhsT=qk[:, h, :], rhs=Sbf[:, go + h, :],
                                                 start=False, stop=True)
                            return r
                        S.op("pe", fo_, reads=["scm" + g, "v" + g + P, "Sbf" + g, "zerob"] + qk_reads, writes=[PK(bo)])
                        if kind == "P":
                            S.op("act", lambda e: e.activation(out=Sbf[:, go:go + 4, :], in_=Sm[:, go:go + 4, :], func=AF.Copy),
                                 reads=["Sm" + g], writes=["Sbf" + g])
                            norm_and_gate(g)
                        yield
                    def norm_and_gate(g):
                        go = 0 if g == "A" else 4
                        bo = bos[g]
                        for h in range(4):
                            S.op("dve", lambda e, h=h: e.bn_stats(out=st6[:, go + h, :], in_=psb[bo][:, h * 128:(h + 1) * 128]),
                                 reads=[PK(bo)], writes=["st6" + g])
                        for h in range(4):
                            S.op("dve", lambda e, h=h: e.bn_aggr(out=mv[:, go + h, :], in_=st6[:, go + h, :]),
                                 reads=["st6" + g], writes=["mv" + g])
                        mean = mv[:, go:go + 4, 0]
                        var = mv[:, go:go + 4, 1]
                        s1 = sm1[:, go:go + 4]
                        rs = rstd[:, go:go + 4]
                        if g == "A":
                            S.op("dve", lambda e: e.tensor_tensor(out=s1, in0=mean, in1=mean, op=ALU.mult),
                                 reads=["mv" + g], writes=["sm1" + g])
                            S.op("dve", lambda e: e.scalar_tensor_tensor(out=s1, in0=s1, scalar=EPS, in1=var,
                                                                         op0=ALU.add, op1=ALU.add),
                                 reads=["mv" + g, "sm1" + g], writes=["sm1" + g])
                        else:
                            S.op("dve", lambda e: e.tensor_scalar(out=s1, in0=var, scalar1=EPS, scalar2=None, op0=ALU.add),
                                 reads=["mv" + g], writes=["sm1" + g])
                        S.op("pool", lambda e: e.tensor_tensor(out=rs, in0=s1, in1=cm05[:, 0:4], op=ALU.pow),
                             reads=["sm1" + g, "cm05"], writes=["rstd" + g])
                        if g == "A":
                            for h in range(4):
                                S.op("dve", lambda e, h=h: e.scalar_tensor_tensor(
                                    out=mixin[:, h * 128:(h + 1) * 128], in0=psb[bo][:, h * 128:(h + 1) * 128],
                                    scalar=rstd[:, h:h + 1], in1=GA[par][:, h * 128:(h + 1) * 128], op0=ALU.mult, op1=ALU.mult),
                                    reads=[PK(bo), "rstdA", "GA" + P], writes=["mixinA"])
                        else:
                            for h in range(4):
                                hs = slice(h * 128, (h + 1) * 128)
                                S.op("dve", lambda e, h=h, hs=hs: e.scalar_tensor_tensor(
                                    out=GB[par][:, hs], in0=psb[bo][:, hs], scalar=mv[:, 4 + h, 0:1], in1=GB[par][:, hs],
                                    op0=ALU.subtract, op1=ALU.mult),
                                    reads=[PK(bo), "mvB", "GB" + P], writes=["GB" + P])
                            for h in range(4):
                                hs = slice(h * 128, (h + 1) * 128)
                                S.op("dve", lambda e, h=h, hs=hs: e.scalar_tensor_tensor(
                                    out=mixin[:, 512 + h * 128:512 + (h + 1) * 128], in0=GB[par][:, hs],
                                    scalar=rstd[:, 4 + h:5 + h], in1=HB[par][:, hs], op0=ALU.mult, op1=ALU.add),
                                    reads=["GB" + P, "rstdB", "HB" + P], writes=["mixinB"])

                    bos = {}
                    alive = [group("A"), group("B")]
                    while alive:
                        for g_ in list(alive):
                            try:
                                next(g_)
                            except StopIteration:
                                alive.remove(g_)
                        yield
                    boA, boB = bos["A"], bos["B"]

                    if kind == "S":
                        otA, otB = nb(), nb()
                        pinned.add(otA); pinned.add(otB)
                        pins.extend([otA, otB])
                        for j in range(16):
                            s3 = j % NS
                            s2 = j % 2
                            S.op("act", lambda e, s3=s3, s2=s2: e.activation(
                                out=S0bf[s2].rearrange("p a b -> p (a b)"), in_=S0[s3].rearrange("p a b -> p (a b)"),
                                func=AF.Copy),
                                reads=S0K[s3], writes=S0bfK[s2])

                            def fot(e, j=j, s2=s2):
                                r = None
                                for h in range(4):
                                    r = e.matmul(psb[otA][:, h * 128 + 8 * j:h * 128 + 8 * j + 8], lhsT=S0bf[s2][:, h, :],
                                                 rhs=qkT["A"][:, h, 8 * j:8 * j + 8], start=True, stop=True)
                                for h in range(4):
                                    r = e.matmul(psb[otB][:, h * 128 + 8 * j:h * 128 + 8 * j + 8], lhsT=S0bf[s2][:, 4 + h, :],
                                                 rhs=qkT["B"][:, h, 8 * j:8 * j + 8], start=True, stop=True)
                                return r
                            S.op("pe", fot, reads=S0bfK[s2] + ["qkTA", "qkTB", "qkTBk"], writes=[PK(otA), PK(otB)])
                            S.op("act", lambda e, j=j, s2=s2: e.activation(
                                out=ktem[s2], in_=ktea[:], func=AF.Copy, scale=IND[:, j:j + 1]),
                                reads=["kteA" + P, "kteB" + P, "tab"], writes=ktemK[s2])
                            kvA, kvB = nb(), nb()

                            def fkvs(e, s2=s2, kvA=kvA, kvB=kvB):
                                r = None
                                for h in range(8):
                                    bkx = kvA if h < 4 else kvB
                                    hh = h % 4
                                    r = e.matmul(psb[bkx][:, hh * 128:(hh + 1) * 128], lhsT=ktem[s2][:, h * 128:(h + 1) * 128],
                                                 rhs=va[:, h * 128:(h + 1) * 128], start=True, stop=True)
                                return r
                            S.op("pe", fkvs, reads=ktemK[s2] + ["vA" + P, "vB" + P], writes=[PK(kvA), PK(kvB)])
                            for h in range(8):
                                hh = h % 4
                                bkx = kvA if h < 4 else kvB
                                sc = decA[par][:, hh * 16 + j:hh * 16 + j + 1] if h < 4 else float(GAM[hh] ** 8)
                                S.op("dve", lambda e, h=h, hh=hh, bkx=bkx, sc=sc, s3=s3: e.scalar_tensor_tensor(
                                    out=S0[s3][:, h, :], in0=S0[s3][:, h, :], scalar=sc,
                                    in1=psb[bkx][:, hh * 128:(hh + 1) * 128], op0=ALU.mult, op1=ALU.add),
                                    reads=[PK(bkx), "decA" + P], writes=S0K[s3])
                            S.dma("s0a%d" % s3, lambda e, j=j, s3=s3: e.dma_start(
                                out=nsab_s[j], in_=S0[s3].rearrange("p a b -> p (a b)")), writes=S0K[s3])
                            if j + NS < 16:
                                load_state(j + NS)
                            yield
                        for g, ot, bo, osb, okey in (("A", otA, boA, T[0], "T0"), ("B", otB, boB, T[1], "T1")):
                            S.op("act", lambda e, ot=ot, osb=osb: e.activation(out=osb, in_=psb[ot][:], func=AF.Copy),
                                 reads=[PK(ot)], writes=[okey])

                            def facc(e, bo=bo, osb=osb):
                                r = None
                                for h in range(4):
                                    r = e.matmul(psb[bo][:, h * 128:(h + 1) * 128], lhsT=osb[:, h * 128:(h + 1) * 128],
                                                 rhs=identf, start=False, stop=True, skip_group_check=True)
                                return r
                            S.op("pe", facc, reads=[okey, "tab"], writes=[PK(bo)])
                        yield

                    if kind == "S":
                        norm_and_gate("A")
                        norm_and_gate("B")
                    for b_ in pins:
                        pinned.discard(b_)
                    yield "TAIL"


                    bk = nb()
                    pbm = psb[bk][:].bitcast(BF16)

                    def fmt(e):
                        r = None
                        for kq in range(8):
                            r = e.transpose(pbm[:, kq * 128:(kq + 1) * 128], mixin[:, kq * 128:(kq + 1) * 128], identb[:])
                        return r
                    S.op("pe", fmt, reads=["mixinA", "mixinB", "identb"], writes=[PK(bk)])
                    S.op("act", lambda e: e.activation(out=hT[:, :, i * 128:(i + 1) * 128],
                                                       in_=pbm.rearrange("p (a b) -> p a b", a=8), func=AF.Copy),
                         reads=[PK(bk)], writes=[("hT", i), "hTalias"])
                    yield
                    if i == 15:
                        S.dma("smA", lambda e: e.dma_start(out=nsa_p.rearrange("h k v -> k h v"), in_=Sm[:, 0:4, :]), reads=["SmA"])
                        S.dma("smB", lambda e: e.dma_start(out=nsb_p.rearrange("h k v -> k h v"), in_=Sm[:, 4:8, :]), reads=["SmB"])

                f0 = front(0)
                next(f0); next(f0)
                for _ in win_loader():
                    pass
                run_interleaved([f0])
                for i in range(NT):
                    if i == 2:
                        for k in range(8):
                            load_cast(w16[:, k, :], w_out[k * 128:(k + 1) * 128, :], (1, 1024), "w16", eng="pool")
                    if i + 1 < NT:
                        run_pair(back(i), front(i + 1))
                    else:
                        def ffn_prefetch():
                            for G in (0, 1):
                                yield from gen_load_ffn_group(G, ["pool"], extra=WIN_KEYS)
                        run_interleaved([back(i), ffn_prefetch()])
                S.wait_all_dma()
                S.flush(block)
            S.after_barrier()

        s2 = contextlib.ExitStack()
        with s2:
            R = sbt(s2, "R", [128, NT, D])
            st12 = [sbt(s2, "st12_%d" % i, [128, 2, 6]) for i in range(2)]
            mv2 = [sbt(s2, "mv2_%d" % i, [128, 2]) for i in range(2)]
            t2a = [sbt(s2, "t2a%d" % i, [128, 1]) for i in range(2)]
            rs2 = [sbt(s2, "rs2_%d" % i, [128, 1]) for i in range(2)]
            nm2 = [sbt(s2, "nm2_%d" % i, [128, 1]) for i in range(2)]
            g_bc = sbt(s2, "g_bc", [128, D]); b_bc = sbt(s2, "b_bc", [128, D])

            def ln_stats(src_ap, src_keys, par):
                P = str(par)
                for half in range(2):
                    S.op("dve", lambda e, half=half: e.bn_stats(out=st12[par][:, half, :], in_=src_ap[:, half * 512:(half + 1) * 512]),
                         reads=src_keys, writes=["st12" + P])
                S.op("dve", lambda e: e.bn_aggr(out=mv2[par][:], in_=st12[par][:].rearrange("p a b -> p (a b)")),
                     reads=["st12" + P], writes=["mv2" + P])
                S.op("dve", lambda e: e.tensor_scalar(out=t2a[par][:], in0=mv2[par][:, 1:2], scalar1=EPS, scalar2=None, op0=ALU.add),
                     reads=["mv2" + P], writes=["t2a" + P])
                S.op("pool", lambda e: e.tensor_tensor(out=rs2[par][:], in0=t2a[par][:], in1=cm05[:, 0:1], op=ALU.pow),
                     reads=["t2a" + P, "cm05"], writes=["rs2" + P])

            def ln_apply(src_ap, src_keys, dst_ap, dst_key, par, eng2="dve"):
                P = str(par)
                S.op("dve", lambda e: e.scalar_tensor_tensor(out=dst_ap, in0=src_ap, scalar=mv2[par][:, 0:1], in1=g_bc[:],
                                                             op0=ALU.subtract, op1=ALU.mult),
                     reads=src_keys + ["mv2" + P, "g_bc"], writes=[dst_key])
                S.op("dve", lambda e: e.scalar_tensor_tensor(out=dst_ap, in0=dst_ap, scalar=rs2[par][:, 0:1], in1=b_bc[:],
                                                             op0=ALU.mult, op1=ALU.add),
                     reads=[dst_key, "rs2" + P, "b_bc"], writes=[dst_key])

            def transposes_f32(src_ap, srckeys, n, dst_fn, dstkeys, scale=None):
                for c0 in range(0, n, 4):
                    cnt = min(4, n - c0)
                    bk = nb()

                    def f(e, bk=bk, c0=c0, cnt=cnt):
                        r = None
                        for q in range(cnt):
                            r = e.transpose(psb[bk][:, q * 128:(q + 1) * 128], src_ap[:, (c0 + q) * 128:(c0 + q + 1) * 128], identf)
                        return r
                    S.op("pe", f, reads=srckeys + ["tab"], writes=[PK(bk)])
                    dst = dst_fn(c0, c0 + cnt)
                    srcv = psb[bk][:, 0:cnt * 128].rearrange("p (a b) -> p a b", a=cnt)
                    sc = 1.0 if scale is None else float(scale)
                    if (c0 // 4) % 2 == 0:
                        S.op("act", lambda e, dst=dst, srcv=srcv, sc=sc: e.activation(out=dst, in_=srcv, func=AF.Copy, scale=sc),
                             reads=[PK(bk)], writes=[dstkeys[c0 // 4]])
                    else:
                        S.op("dve", lambda e, dst=dst, srcv=srcv, sc=sc: e.tensor_scalar(out=dst, in0=srcv, scalar1=sc, scalar2=None, op0=ALU.mult),
                             reads=[PK(bk)], writes=[dstkeys[c0 // 4]])

            sab = contextlib.ExitStack()
            with sab:
                sgt = [sbt(sab, "sgt%d" % i, [128, 512]) for i in range(2)]
                cast_rr = [0]

                with nc.Block(no_gpsimd_drain=True) as block:
                    S.dma("gbc", lambda e: e.dma_start(out=g_bc[:], in_=ln1g.partition_broadcast(128)), writes=["g_bc"])
                    S.dma("bbc", lambda e: e.dma_start(out=b_bc[:], in_=ln1b.partition_broadcast(128)), writes=["b_bc"])
                    S.dma("x0", lambda e: e.dma_start(out=xsl[0][:], in_=x_ap(0)), writes=[("xsl", 0)])
                    S.op("pool", lambda e: e.tensor_scalar(out=g_bc[:], in0=g_bc[:], scalar1=float(ALPHA), scalar2=None, op0=ALU.mult),
                         reads=["g_bc"], writes=["g_bc"])
                    S.op("pool", lambda e: e.tensor_scalar(out=b_bc[:], in0=b_bc[:], scalar1=float(ALPHA), scalar2=None, op0=ALU.mult),
                         reads=["b_bc"], writes=["b_bc"])

                    def a2_s12(i):
                        sl = i % 2
                        par = i % 2
                        if i + 1 < NT:
                            S.dma("x%d" % ((i + 1) % 2), lambda e: e.dma_start(out=xsl[(i + 1) % 2][:], in_=x_ap(i + 1)),
                                  writes=[("xsl", (i + 1) % 2)])
                        for half in range(2):
                            bk = nb()

                            def f(e, bk=bk, half=half):
                                r = None
                                for k in range(8):
                                    r = e.matmul(psb[bk][:], lhsT=hT[:, k, i * 128:(i + 1) * 128],
                                                 rhs=w16[:, k, half * 512:(half + 1) * 512], start=(k == 0), stop=(k == 7))
                                return r
                            S.op("pe", f, reads=[("hT", i), "w16"], writes=[PK(bk)])
                            S.op("dve", lambda e, bk=bk, half=half: e.scalar_tensor_tensor(
                                out=r_t[par][:, half * 512:(half + 1) * 512], in0=xsl[sl][:, half * 512:(half + 1) * 512],
                                scalar=float(ALPHA), in1=psb[bk][:], op0=ALU.mult, op1=ALU.add),
                                reads=[PK(bk), ("xsl", sl)], writes=[("r_t", par)])

                    def a2_stats(i):
                        par = i % 2
                        ln_stats(r_t[par][:], [("r_t", par)], par)

                    def a2_apply(i):
                        par = i % 2
                        ln_apply(r_t[par][:], [("r_t", par)], R[:, i, :], ("R", i), par)

                    def a2_block(tl):
                        for p0 in range(0, len(tl), 2):
                            pair = tl[p0:p0 + 2]
                            for t in pair:
                                a2_s12(t)
                            for t in pair:
                                a2_stats(t)
                            for t in pair:
                                a2_apply(t)

                    def a2_s3(i):
                        transposes_f32(R[:, i, :], [("R", i)], 8, lambda lo, hi: hT[:, lo:hi, i * 128:(i + 1) * 128],
                                       [("hT", i), ("hT", i)], scale=1.0 / ALPHA)

                    blocks = [(0, 512), (512, 512), (1024, 512), (1536, 512), (2048, 128)]
                    btiles = [list(range(t0 // 128, (t0 + n) // 128)) for (t0, n) in blocks]
                    a2_block(btiles[0])
                    for t in btiles[0]:
                        a2_s3(t)
                    abi = [0]
                    sgi = [0]
                    ln2_pending = []
                    for G in range(NG):
                        slw = G % 2
                        nch = GRP[G][1]
                        for bi, (t0, n) in enumerate(blocks):
                            ab = abi[0] % 2
                            abi[0] += 1
                            tiles = btiles[bi]
                            hkeys = [("hT", t) for t in tiles]
                            if G == 0 and bi + 1 < len(blocks):
                                a2_block(btiles[bi + 1])
                            for c in range(nch):
                                bg, bu = nb(), nb()

                                def fg(e, bg=bg, c=c, t0=t0, n=n, slw=slw):
                                    r = None
                                    for k in range(8):
                                        r = e.matmul(psb[bg][:, 0:n], lhsT=wgs[slw][:, k, c * 128:(c + 1) * 128],
                                                     rhs=hT[:, k, t0:t0 + n], start=(k == 0), stop=(k == 7))
                                    return r

                                def fu(e, bu=bu, c=c, t0=t0, n=n, slw=slw):
                                    r = None
                                    for k in range(8):
                                        r = e.matmul(psb[bu][:, 0:n], lhsT=wus[slw][:, k, c * 128:(c + 1) * 128],
                                                     rhs=hT[:, k, t0:t0 + n], start=(k == 0), stop=(k == 7))
                                    return r
                                S.op("pe", fg, reads=hkeys + [("wg", slw)], writes=[PK(bg)])
                                S.op("pe", fu, reads=hkeys + [("wu", slw)], writes=[PK(bu)])
                                sg = sgi[0] % 2
                                sgi[0] += 1
                                S.op("act", lambda e, bg=bg, sg=sg, n=n: e.activation(out=sgt[sg][:, 0:n], in_=psb[bg][:, 0:n], func=AF.Silu),
                                     reads=[PK(bg)], writes=[("sgt", sg)])
                                S.op("dve", lambda e, bu=bu, sg=sg, n=n, ab=ab, c=c: e.tensor_tensor(
                                    out=actb[ab][:, c, 0:n], in0=sgt[sg][:, 0:n], in1=psb[bu][:, 0:n], op=ALU.mult),
                                    reads=[PK(bu), ("sgt", sg)], writes=[("actb", ab, c)])
                            for ti, t in enumerate(tiles):
                                for half in range(2):
                                    bd = nb()

                                    def fd(e, bd=bd, ti=ti, half=half, ab=ab, slw=slw, nch=nch):
                                        r = None
                                        for c in range(nch):
                                            r = e.matmul(psb[bd][:], lhsT=actb[ab][:, c, ti * 128:(ti + 1) * 128],
                                                         rhs=wds[slw][:, c, half * 512:(half + 1) * 512],
                                                         start=(c == 0), stop=(c == nch - 1))
                                        return r
                                    S.op("pe", fd, reads=[("actb", ab, c) for c in range(nch)] + [("wd", slw)], writes=[PK(bd)])
                                    S.op("dve", lambda e, bd=bd, t=t, half=half: e.tensor_tensor(
                                        out=R[:, t, half * 512:(half + 1) * 512], in0=psb[bd][:],
                                        in1=R[:, t, half * 512:(half + 1) * 512], op=ALU.add),
                                        reads=[PK(bd), ("R", t)], writes=[("R", t)])
                                if G == NG - 1:
                                    ln_stats(R[:, t, :], [("R", t)], t % 2)
                                    if ln2_pending:
                                        tp_ = ln2_pending.pop()
                                        ln_apply(R[:, tp_, :], [("R", tp_)], R[:, tp_, :], ("R", tp_), tp_ % 2)
                                    ln2_pending.append(t)
                            if G == 0 and bi + 1 < len(blocks):
                                for t in btiles[bi + 1]:
                                    a2_s3(t)
                        if G == 0:
                            for k in range(8):
                                load_cast(w16[:, k, :], wpg[k * 128:(k + 1) * 128, :], (1, 1024), "w16", eng="pool")
                            S.dma("gbc", lambda e: e.dma_start(out=g_bc[:], in_=ln2g.partition_broadcast(128)), writes=["g_bc"])
                            S.dma("bbc", lambda e: e.dma_start(out=b_bc[:], in_=ln2b.partition_broadcast(128)), writes=["b_bc"])
                        if G + 2 < NG:
                            load_ffn_group(G + 2, ["pool"])
                    for tp_ in ln2_pending:
                        ln_apply(R[:, tp_, :], [("R", tp_)], R[:, tp_, :], ("R", tp_), tp_ % 2)

                    wpp_bf = wbig[:, 0:2048].rearrange("p (k c) -> p k c", k=2)
                    bpg_bc = wbig[:, 2048:4096].bitcast(F32)
                    h2T = [wbig[:, 4096 + i * 1024:4096 + (i + 1) * 1024].rearrange("p (k c) -> p k c", k=8) for i in range(2)]
                    pT = [wbig[:, 6144 + i * 256:6144 + (i + 1) * 256].rearrange("p (k c) -> p k c", k=2) for i in range(2)]
                    pt = [wbig[:, 6656 + i * 512:6656 + (i + 1) * 512].bitcast(F32) for i in range(2)]
                    gs = [wbig[:, 8192 + i * 2048:8192 + (i + 1) * 2048].bitcast(F32) for i in range(2)]
                    ybuf = [wbig[:, 24576 + i * 2048:24576 + (i + 1) * 2048].bitcast(F32) for i in range(2)]
                    for k in range(2):
                        load_cast(wpp_bf[:, k, :], wpp[k * 128:(k + 1) * 128, :], (1, 1024), "wpp", eng="act", wkeys_extra=SLOT0)
                    S.dma("bpg", lambda e: e.dma_start(out=bpg_bc[:], in_=bpg.partition_broadcast(128)), writes=["bpg_bc"] + SLOT0)
                    S.dma("p0", lambda e: e.dma_start(out=pt[0][:], in_=p_ap(0)), writes=[("pt", 0)] + SLOT0)

                    def c_s2(i):
                        par = i % 2
                        if i + 1 < NT:
                            S.dma("p%d" % ((i + 1) % 2), lambda e: e.dma_start(out=pt[(i + 1) % 2][:], in_=p_ap(i + 1)),
                                  writes=[("pt", (i + 1) % 2)] + SLOT0)
                        transposes_f32(R[:, i, :], [("R", i)], 8, lambda lo, hi: h2T[par][:, lo:hi, :],
                                       [("h2T", par, 0), ("h2T", par, 1)])
                        transposes_f32(pt[par][:], [("pt", par)], 2, lambda lo, hi: pT[par][:, lo:hi, :], [("pT", par)])

                    def c_s3(i):
                        par = i % 2
                        sl = par
                        for half in range(2):
                            bgt, bpp = nb(), nb()

                            def fgt(e, bgt=bgt, half=half):
                                r = None
                                for k in range(8):
                                    r = e.matmul(psb[bgt][:], lhsT=h2T[par][:, k, :], rhs=w16[:, k, half * 512:(half + 1) * 512],
                                                 start=(k == 0), stop=(k == 7))
                                return r

                            def fpp(e, bpp=bpp, half=half):
                                r = None
                                for k in range(2):
                                    r = e.matmul(psb[bpp][:], lhsT=pT[par][:, k, :], rhs=wpp_bf[:, k, half * 512:(half + 1) * 512],
                                                 start=(k == 0), stop=(k == 1))
                                return r
                            S.op("pe", fgt, reads=[("h2T", par, 0), ("h2T", par, 1), "w16"], writes=[PK(bgt)])
                            S.op("pe", fpp, reads=[("pT", par), "wpp"], writes=[PK(bpp)])
                            hs = slice(half * 512, (half + 1) * 512)
                            gk = ("gs", par, half)
                            S.op("dve", lambda e, bgt=bgt, hs=hs: e.tensor_tensor(out=gs[par][:, hs], in0=psb[bgt][:], in1=bpg_bc[:, hs], op=ALU.add),
                                 reads=[PK(bgt), "bpg_bc"], writes=[gk])
                            S.op("act", lambda e, hs=hs: e.activation(out=gs[par][:, hs], in_=gs[par][:, hs], func=AF.Sigmoid),
                                 reads=[gk], writes=[gk])
                            S.op("dve", lambda e, bpp=bpp, hs=hs: e.tensor_tensor(out=gs[par][:, hs], in0=gs[par][:, hs], in1=psb[bpp][:], op=ALU.mult),
                                 reads=[PK(bpp), gk], writes=[gk])
                            S.op("pool", lambda e, hs=hs: e.tensor_tensor(out=ybuf[sl][:, hs], in0=gs[par][:, hs], in1=R[:, i, hs], op=ALU.add),
                                 reads=[gk, ("R", i)], writes=[("ybuf", sl)])
                        S.dma("y%d" % sl, lambda e: e.dma_start(out=y_ap(i), in_=ybuf[sl][:]), reads=[("ybuf", sl)])

                    c_s2(0)
                    for i in range(NT):
                        if i + 1 < NT:
                            c_s2(i + 1)
                        c_s3(i)
                    S.wait_all_dma()
                    S.flush(block)
                S.after_barrier()

    print("sched: ops", S.nops, "waits", S.nwaits, flush=True)
    return nc


_CACHE = {}


def make_in_maps(inputs):
    tab, csn = host_tables()
    f = lambda a: np.ascontiguousarray(np.asarray(a, dtype=np.float32))
    maps = []
    for c in range(NCORES):
        m = {
            "xp": f(inputs["x_prompt"][c]), "xs": f(inputs["x_sample"][16 * c:16 * c + 16].reshape(128, D)),
            "pp": f(inputs["p_prompt"][0, c]), "ps": f(inputs["p_sample"][0, 16 * c:16 * c + 16].reshape(128, 256)),
            "sab": f(np.concatenate([inputs["state_hgrn"][0, 16 * c:16 * c + 16], inputs["state_ret"][0, 16 * c:16 * c + 16]],
                                    axis=1).transpose(0, 2, 1, 3).reshape(16, 128, 1024)),
            "lbl": f(inputs["lb_logits"]), "w_in": f(inputs["w_in"][0]), "w_out": f(inputs["w_out"][0]),
            "ang": f(inputs["a_norm_g"][0]), "bng": f(inputs["b_norm_g"][0]), "bnb": f(inputs["b_norm_b"][0]),
            "ln1g": f(inputs["ln1_g"][0]), "ln1b": f(inputs["ln1_b"][0]), "ln2g": f(inputs["ln2_g"][0]), "ln2b": f(inputs["ln2_b"][0]),
            "wg": f(inputs["w_ffn_gate"][0]), "wu": f(inputs["w_ffn_up"][0]), "wd": f(inputs["w_ffn_down"][0]),
            "wpp": f(inputs["w_ple_proj"][0]), "wpg": f(inputs["w_ple_gate"][0]), "bpg": f(inputs["b_ple_gate"][0]),
            "tab": tab, "csn": csn,
        }
        maps.append(m)
    return maps


def kernel(**inputs):
    if "nc" not in _CACHE:
        _CACHE["nc"] = build(debug=False)
    nc = _CACHE["nc"]
    maps = make_in_maps(inputs)
    res = run_bass_kernel_spmd(nc, maps, core_ids=list(range(NCORES)))
    rs = res.results
    y_prompt = np.stack([rs[c]["yp"] for c in range(NCORES)], 0).astype(np.float32)
    y_sample = np.concatenate([rs[c]["ys"].reshape(16, 8, D) for c in range(NCORES)], 0).astype(np.float32)
    nsa_p = np.stack([rs[c]["nsa_p"] for c in range(NCORES)], 0)[None].astype(np.float32)
    nsb_p = np.stack([rs[c]["nsb_p"] for c in range(NCORES)], 0)[None].astype(np.float32)
    nsab = np.concatenate([rs[c]["nsab_s"] for c in range(NCORES)], 0).reshape(128, 128, 8, 128).transpose(0, 2, 1, 3)
    nsa_s = np.ascontiguousarray(nsab[:, 0:4])[None].astype(np.float32)
    nsb_s = np.ascontiguousarray(nsab[:, 4:8])[None].astype(np.float32)
    return (y_prompt, y_sample, nsa_p, nsb_p, nsa_s, nsb_s)
```
